# Optimizing a Trainium2 kernel written in Bass

```python
import jax, jax.numpy as jnp
from jax import lax
import numpy as np

D_MODEL = 2048
BATCH = 1
SEQ = 8192
DEPTH = 1
DEC_BATCH = 32
DEC_SEQ = 16
PAST_LEN = 4096

CHUNK = 64
N_META = 16
CONV_CH = D_MODEL // 2
POOL_CH = D_MODEL - CONV_CH
MIX_WIDTH = CONV_CH + POOL_CH
CONV_K = 31
POOL_WINDOWS = (2, 4, 8, 16)
N_POOL_GROUPS = len(POOL_WINDOWS)
POOL_GROUP = POOL_CH // N_POOL_GROUPS
POOL_HIST = max(POOL_WINDOWS) - 1
D_FF = ((8 * D_MODEL // 3 + 255) // 256) * 256
EPS = 1e-6

kernel_name = 'hybrid_conv_pool_streaming_encoder_step'


def rmsnorm(x, g):
    xf = x.astype(jnp.float32)
    y = xf * lax.rsqrt(jnp.mean(xf * xf, axis=-1, keepdims=True) + EPS)
    return (y * g.astype(jnp.float32)).astype(x.dtype)


def swiglu(x, w_gate, w_up, w_down):
    return (jax.nn.silu(x @ w_gate) * (x @ w_up)) @ w_down


def causal_dwconv(u_ext, w, b):
    c = u_ext.shape[-1]
    y = lax.conv_general_dilated(
        u_ext, w[:, None, :].astype(u_ext.dtype), window_strides=(1,), padding='VALID',
        dimension_numbers=('NWC', 'WIO', 'NWC'), feature_group_count=c)
    return y + b.astype(y.dtype)


def multiscale_pool(u_ext, n_hist):
    total = u_ext.shape[1]
    uf = u_ext.astype(jnp.float32)
    cs = jnp.concatenate([jnp.zeros_like(uf[:, :1]), jnp.cumsum(uf, axis=1)], axis=1)
    idx = np.arange(n_hist, total)
    cs_end = cs[:, n_hist + 1:]
    outs = []
    for g, w in enumerate(POOL_WINDOWS):
        lo = np.maximum(idx + 1 - w, 0)
        count = jnp.asarray((idx + 1 - lo).astype(np.float32))[None, :, None]
        sl = slice(g * POOL_GROUP, (g + 1) * POOL_GROUP)
        window_sum = cs_end[:, :, sl] - jnp.take(cs[:, :, sl], jnp.asarray(lo), axis=1)
        outs.append(window_sum / count - uf[:, n_hist:, sl])
    return jnp.stack(outs, axis=2).astype(u_ext.dtype)


def encoder_layer(x, conv_hist, pool_hist,
                  ffn1_norm, ffn1_w_gate, ffn1_w_up, ffn1_w_down,
                  mix_norm, w_in, conv_w, conv_b, conv_norm, pool_w, pool_scale, w_out,
                  ffn2_norm, ffn2_w_gate, ffn2_w_up, ffn2_w_down):
    b, l, _ = x.shape
    h = x + 0.5 * swiglu(rmsnorm(x, ffn1_norm), ffn1_w_gate, ffn1_w_up, ffn1_w_down)
    z = rmsnorm(h, mix_norm) @ w_in
    a = z[..., :CONV_CH]
    gate = z[..., CONV_CH:2 * CONV_CH]
    u_pool = z[..., 2 * CONV_CH:]
    u_conv = a * jax.nn.sigmoid(gate)
    conv_ext = jnp.concatenate([conv_hist.astype(u_conv.dtype), u_conv], axis=1)
    c = jax.nn.silu(rmsnorm(causal_dwconv(conv_ext, conv_w, conv_b), conv_norm))
    pool_ext = jnp.concatenate([pool_hist.astype(u_pool.dtype), u_pool], axis=1)
    p = multiscale_pool(pool_ext, pool_hist.shape[1])
    p = jnp.einsum('blgc,gcd->blgd', p, pool_w).reshape(b, l, POOL_CH) * pool_scale
    h = h + jnp.concatenate([c, p], axis=-1) @ w_out
    h = h + 0.5 * swiglu(rmsnorm(h, ffn2_norm), ffn2_w_gate, ffn2_w_up, ffn2_w_down)
    return h, conv_ext[:, -(CONV_K - 1):], pool_ext[:, -POOL_HIST:]


def setup_inputs(seed: int = 0) -> dict:
    key = jax.random.key(seed)
    ks = jax.random.split(key, 24)

    def nrm(k, shape, fan_in):
        return jax.random.normal(k, shape, jnp.float32) * (fan_in ** -0.5)

    def gain(k, shape):
        return 1.0 + 0.05 * jax.random.normal(k, shape, jnp.float32)

    return {
        'x_prompt': jax.random.normal(ks[0], (BATCH, SEQ, D_MODEL), jnp.float32),
        'x_sample': jax.random.normal(ks[1], (DEC_BATCH, DEC_SEQ, D_MODEL), jnp.float32),
        'state_conv': 0.5 * jax.random.normal(ks[2], (DEPTH, DEC_BATCH, CONV_K - 1, CONV_CH), jnp.float32),
        'state_pool': jax.random.normal(ks[3], (DEPTH, DEC_BATCH, POOL_HIST, POOL_CH), jnp.float32),
        'meta_tokens': jax.random.normal(ks[4], (N_META, D_MODEL), jnp.float32),
        'ffn1_norm': gain(ks[5], (DEPTH, D_MODEL)),
        'ffn1_w_gate': nrm(ks[6], (DEPTH, D_MODEL, D_FF), D_MODEL),
        'ffn1_w_up': nrm(ks[7], (DEPTH, D_MODEL, D_FF), D_MODEL),
        'ffn1_w_down': nrm(ks[8], (DEPTH, D_FF, D_MODEL), D_FF),
        'mix_norm': gain(ks[9], (DEPTH, D_MODEL)),
        'w_in': nrm(ks[10], (DEPTH, D_MODEL, 2 * CONV_CH + POOL_CH), D_MODEL),
        'conv_w': nrm(ks[11], (DEPTH, CONV_K, CONV_CH), CONV_K),
        'conv_b': 0.02 * jax.random.normal(ks[12], (DEPTH, CONV_CH), jnp.float32),
        'conv_norm': gain(ks[13], (DEPTH, CONV_CH)),
        'pool_w': nrm(ks[14], (DEPTH, N_POOL_GROUPS, POOL_GROUP, POOL_GROUP), POOL_GROUP),
        'pool_scale': gain(ks[15], (DEPTH, POOL_CH)),
        'w_out': nrm(ks[16], (DEPTH, MIX_WIDTH, D_MODEL), MIX_WIDTH),
        'ffn2_norm': gain(ks[17], (DEPTH, D_MODEL)),
        'ffn2_w_gate': nrm(ks[18], (DEPTH, D_MODEL, D_FF), D_MODEL),
        'ffn2_w_up': nrm(ks[19], (DEPTH, D_MODEL, D_FF), D_MODEL),
        'ffn2_w_down': nrm(ks[20], (DEPTH, D_FF, D_MODEL), D_FF),
        'final_norm': gain(ks[21], (D_MODEL,)),
    }


def reference(x_prompt, x_sample, state_conv, state_pool, meta_tokens,
              ffn1_norm, ffn1_w_gate, ffn1_w_up, ffn1_w_down,
              mix_norm, w_in, conv_w, conv_b, conv_norm, pool_w, pool_scale, w_out,
              ffn2_norm, ffn2_w_gate, ffn2_w_up, ffn2_w_down, final_norm):
    b_p = x_prompt.shape[0]
    meta = jnp.broadcast_to(meta_tokens.astype(x_prompt.dtype)[None], (b_p, N_META, D_MODEL))
    h_p = jnp.concatenate([meta, x_prompt], axis=1)
    h_s = x_sample
    conv_p_list, pool_p_list, conv_s_list, pool_s_list = [], [], [], []
    for d in range(DEPTH):
        params = (ffn1_norm[d], ffn1_w_gate[d], ffn1_w_up[d], ffn1_w_down[d],
                  mix_norm[d], w_in[d], conv_w[d], conv_b[d], conv_norm[d], pool_w[d], pool_scale[d], w_out[d],
                  ffn2_norm[d], ffn2_w_gate[d], ffn2_w_up[d], ffn2_w_down[d])
        conv_pad = jnp.zeros((b_p, CONV_K - 1, CONV_CH), h_p.dtype)
        pool_none = jnp.zeros((b_p, 0, POOL_CH), h_p.dtype)
        h_p, cp, pp = encoder_layer(h_p, conv_pad, pool_none, *params)
        h_s, cs_, ps_ = encoder_layer(h_s, state_conv[d], state_pool[d], *params)
        conv_p_list.append(cp)
        pool_p_list.append(pp)
        conv_s_list.append(cs_)
        pool_s_list.append(ps_)
    y_prompt = rmsnorm(h_p, final_norm)[:, N_META:]
    y_sample = rmsnorm(h_s, final_norm)
    new_conv_prompt = jnp.stack(conv_p_list, axis=0)
    new_pool_prompt = jnp.stack(pool_p_list, axis=0)
    new_conv_sample = jnp.stack(conv_s_list, axis=0)
    new_pool_sample = jnp.stack(pool_s_list, axis=0)
    return (y_prompt, y_sample, new_conv_prompt, new_pool_prompt, new_conv_sample, new_pool_sample)
```

```python
from contextlib import ExitStack

import numpy as np

import concourse.bass as bass
import concourse.mybir as mybir
from concourse.bass_utils import run_bass_kernel_spmd

F32 = mybir.dt.float32
BF16 = mybir.dt.bfloat16
F32R = mybir.dt.float32r
AF = mybir.ActivationFunctionType
ALU = mybir.AluOpType

NCORES = 8
D = 2048
KC = D // 128
CCH = 8
PCH = 8
HALO = 30
CONV_K = 31
PHIST = 15
NSTR = 4
DSEQ = 16
NS = NSTR * DSEQ
LC = HALO + DSEQ
LP = PHIST + DSEQ
POOL_W = (2, 4, 8, 16)
N_META = 16
EPS = 1e-6
GJ = 4
R = 4
ENGS = ("pe", "act", "dve", "pool", "sp")


def _split(a, b, maxtile):
    n = b - a
    u = 2 if n % 2 == 0 else 1
    k = -(-n // maxtile)
    base, rem = divmod(n // u, k)
    out, s = [], a
    for i in range(k):
        w = u * (base + (1 if i < rem else 0))
        out.append((s, s + w))
        s += w
    assert s == b
    return out


class Cfg:
    def __init__(self, dff=5632, nown=1026, maxtile=512):
        self.dff = dff
        self.jc = dff // 128
        assert self.jc % GJ == 0
        self.ng = self.jc // GJ
        self.nown = nown
        self.PW = HALO + nown
        self.N1 = self.PW + NS
        self.N2 = nown + NS
        self.UC = self.PW + NSTR * LC
        self.UP = self.PW + NSTR * LP
        self.CV = self.UC - HALO
        self.T1 = _split(0, self.N1, maxtile)
        self.T2 = _split(HALO, self.N1, maxtile)
        self.C2 = [(a - HALO, b - HALO) for a, b in self.T2]
        for tl in (self.T1, self.T2):
            assert tl[-1][0] <= self.PW and all(t[1] <= self.PW for t in tl[:-1])
        assert self.N1 <= 1240 and self.CV >= self.N1


class Tok:
    __slots__ = ("sem", "val", "key")

    def __init__(self, sem, val, key):
        self.sem, self.val, self.key = sem, val, key


class Prog:
    def __init__(self, nc, stack):
        self.nc = nc
        self.stack = stack
        self.q = {e: [] for e in ENGS}
        self.clk = {e: stack.enter_context(nc.semaphore("clk_" + e)) for e in ENGS}
        self.cnt = {e: 0 for e in ENGS}
        self.waited = {e: {} for e in ENGS}
        self.nsem = 0

    def new_sem(self, name):
        self.nsem += 1
        return self.stack.enter_context(self.nc.semaphore(name))

    def wait(self, eng, *toks):
        for t in toks:
            if t is None:
                continue
            if isinstance(t, (list, tuple)):
                self.wait(eng, *t)
                continue
            w = self.waited[eng]
            if w.get(t.key, 0) >= t.val:
                continue
            w[t.key] = t.val
            self.q[eng].append(("wait", t.sem, t.val))

    def op(self, eng, fn, sig=False, waits=()):
        self.wait(eng, *waits)
        if sig:
            self.cnt[eng] += 1
            self.q[eng].append(("op", fn, self.clk[eng]))
            return Tok(self.clk[eng], self.cnt[eng], eng)
        self.q[eng].append(("op", fn, None))
        return None

    def dma(self, eng, out, in_, sem, val, waits=()):
        self.wait(eng, *waits)
        self.q[eng].append(("dma", out, in_, sem))
        return Tok(sem, val, ("dma", id(sem)))

    def run(self, block):
        handles = {"pe": "tensor", "act": "scalar", "dve": "vector", "pool": "gpsimd", "sp": "sync"}

        def make(e):
            items = self.q[e]

            def body(h):
                for it in items:
                    if it[0] == "wait":
                        h.wait_ge(it[1], it[2])
                    elif it[0] == "op":
                        ins = it[1](h)
                        if it[2] is not None:
                            ins.then_inc(it[2], 1)
                    else:
                        h.dma_start(out=it[1], in_=it[2]).then_inc(it[3], 16)
            return body

        for e in ENGS:
            getattr(block, handles[e])(make(e))


class DmaSem:
    def __init__(self, P, name):
        self.P = P
        self.sem = P.new_sem(name)
        self.n = 0

    def dma(self, eng, out, in_, waits=()):
        self.n += 1
        return self.P.dma(eng, out, in_, self.sem, 16 * self.n, waits=waits)


def mm(P, out, lhsT, rhs, start, stop, sig=False, waits=()):
    return P.op("pe", lambda h: h.matmul(out, lhsT, rhs, start=start, stop=stop), sig=sig, waits=waits)


def act(P, out, in_, func, scale=None, bias=None, waits=(), sig=True):
    kw = {}
    if scale is not None:
        kw["scale"] = scale
    if bias is not None:
        kw["bias"] = bias
    return P.op("act", lambda h: h.activation(out=out, in_=in_, func=func, **kw), sig=sig, waits=waits)


def tt(P, eng, out, in0, in1, op, waits=(), sig=True):
    return P.op(eng, lambda h: h.tensor_tensor(out=out, in0=in0, in1=in1, op=op), sig=sig, waits=waits)


def ts(P, eng, out, in0, s1, s2, op0, op1=None, waits=(), sig=True):
    if op1 is None:
        return P.op(eng, lambda h: h.tensor_scalar(out=out, in0=in0, scalar1=s1, scalar2=None, op0=op0),
                    sig=sig, waits=waits)
    return P.op(eng, lambda h: h.tensor_scalar(out=out, in0=in0, scalar1=s1, scalar2=s2, op0=op0, op1=op1),
                sig=sig, waits=waits)


def stt(P, out, in0, scalar, in1, op0, op1, waits=(), sig=True):
    return P.op("dve", lambda h: h.scalar_tensor_tensor(out=out, in0=in0, scalar=scalar, in1=in1, op0=op0, op1=op1),
                sig=sig, waits=waits)


def v3(ap, s=NSTR):
    return ap.rearrange("p (s l) -> p s l", s=s)


def build_program(cfg):
    N1, N2, PW, UC, UP, CV = cfg.N1, cfg.N2, cfg.PW, cfg.UC, cfg.UP, cfg.CV
    T1, T2, C2 = cfg.T1, cfg.T2, cfg.C2
    JC, NG = cfg.jc, cfg.ng
    nc = bass.Bass("TRN2", target_bir_lowering=False)
    plan = []

    n_w = 2 * (3 * JC) + PCH + NSTR + 2 * CCH + KC

    def din(name, shape, dt=F32):
        return nc.dram_tensor(name, list(shape), dt, kind="ExternalInput").ap()

    def dout(name, shape, dt=F32):
        return nc.dram_tensor(name, list(shape), dt, kind="ExternalOutput").ap()

    xT = din("xT", [128, KC * N1])
    ws = din("ws", [n_w, 128, 2048])
    gains = din("gains", [128, 4 * KC])
    cw = din("cw", [128, CCH * CONV_K])
    cvec = din("cvec", [128, 3 * CCH])
    icnt = din("icnt", [128, 4 * 16])
    ident = din("ident", [128, 128])
    scT = din("scT", [CCH, 128, NSTR * HALO])
    spT = din("spT", [PCH, 128, NSTR * PHIST])
    yT = dout("yT", [KC, 128, N2])
    o_ncp = dout("ncp", [CCH, 128, HALO])
    o_npp = dout("npp", [PCH, 128, PHIST])
    o_ncs = dout("ncs", [CCH, 128, NSTR * HALO])
    o_nps = dout("nps", [PCH, 128, NSTR * PHIST])

    with ExitStack() as st:
        P = Prog(nc, st)

        def sb(name, shape, dt=F32):
            return st.enter_context(nc.sbuf_tensor(name, list(shape), dt))

        h = sb("h", [128, KC, N1])
        xn = sb("xn", [128, KC, N1], BF16)
        hid = sb("hid", [128, 2 * GJ, N1], BF16)
        ring = sb("ring", [128, R, 2048], BF16)
        sqw = max(b_ - a_ for a_, b_ in T1 + T2)
        scratch_end = 4 * sqw + 2 * N1
        cvo_cols = max(CCH * N2, scratch_end + 4 * UP)
        cvo = sb("cvo", [128, cvo_cols])
        ssqc = sb("ssqc", [128, N2])
        dg = sb("dg", [128, 2, 4, 128])
        ident_sb = sb("ident_sb", [128, 128])
        uce = sb("uce", [128, 2, UC])
        pm = sb("pm", [128, 2, N2], BF16)
        pmf = pm[:].rearrange("p a b -> p (a b)").bitcast(F32)
        stmp = sb("stmp", [128, 2, 512])
        statB = sb("statB", [128, N1])
        g_sb = sb("g_sb", [128, 4 * KC])
        cw_sb = sb("cw_sb", [128, CCH * CONV_K])
        cv_sb = sb("cv_sb", [128, 3 * CCH])
        ic_sb = sb("ic_sb", [128, 64])
        ones = sb("ones", [128, 128])
        ones_r = sb("ones_r", [128, 128])
        epsb = sb("epsb", [128, 1])
        tiny = sb("tiny", [128, 16])
        ps = [st.enter_context(nc.psum_tensor(f"ps{i}", [128, 512], F32)) for i in range(8)]

        sqTT = [[cvo[:, i * sqw:(i + 1) * sqw] for i in range(3)],
                [cvo[:, 3 * sqw:4 * sqw]]]
        statAB = [cvo[:, 4 * sqw:4 * sqw + N1], cvo[:, 4 * sqw + N1:4 * sqw + 2 * N1]]
        ptb = cvo_cols - 4 * UP
        assert ptb >= scratch_end
        upes = [cvo[:, ptb:ptb + UP], cvo[:, ptb + UP:ptb + 2 * UP]]
        S_a = cvo[:, ptb + 2 * UP:ptb + 3 * UP]
        S_b = cvo[:, ptb + 3 * UP:ptb + 4 * UP]

        block = st.enter_context(nc.Block())

        slot_sem = [P.new_sem(f"slot{i}") for i in range(R)]
        rel = {}
        load_tok = {}

        def w_issue(i):
            s = i % R
            waits = [rel[i - R]] if i >= R else []
            load_tok[i] = P.dma("pool", ring[:, s, :], ws[i], slot_sem[s], 16 * (i // R + 1), waits=waits)

        def w_release(i, tok):
            rel[i] = tok
            if i + R < n_w and (i + R) not in load_tok:
                w_issue(i + R)

        def w_request(spec):
            i = len(plan)
            plan.append(spec)
            assert i < n_w and i in load_tok, f"ring too small at tile {i}"
            return i, i % R, load_tok[i]

        bank_free = [None] * 8
        acc_rr = [0]

        def next_acc():
            b = 4 + acc_rr[0] % 4
            acc_rr[0] += 1
            return b

        unit_rr = [0]

        def next_pair():
            p = unit_rr[0] % 2
            unit_rr[0] += 1
            return p

        c_ones0 = P.op("pool", lambda hh: hh.memset(ones[:], 1.0), sig=True)
        c_ones = act(P, ones_r[:].bitcast(F32R), ones[:], AF.Copy, waits=[c_ones0])
        c_eps = P.op("pool", lambda hh: hh.memset(epsb[:], EPS), sig=True)
        for i in range(min(R, n_w)):
            w_issue(i)
        psem = DmaSem(P, "psem")
        for dst, src in ((g_sb, gains), (cw_sb, cw), (cv_sb, cvec), (ic_sb, icnt), (ident_sb, ident)):
            par_tok = psem.dma("sp", dst[:], src)
        x_tok = {}
        for ti, (t0, t1) in enumerate(T1):
            src = xT[:, KC * t0:KC * t1].rearrange("p (c t) -> p c t", c=KC)
            hk = KC // 2
            ta = DmaSem(P, f"xs{ti}a").dma("sp", h[:, 0:hk, t0:t1], src[:, 0:hk, :])
            tb = DmaSem(P, f"xs{ti}b").dma("act", h[:, hk:KC, t0:t1], src[:, hk:KC, :])
            x_tok[(t0, t1)] = [ta, tb]

        ssq = {"first": {}, "last": {}, "rd": {}, "n": [0, 0], "guard": None, "pend": [], "dve_only": False}
        ACC_ENG = ("dve", "pool")
        NSQ = (3, 1)

        def ssq_reset(tiles, guard):
            assert not ssq["pend"]
            ssq["first"] = {}
            ssq["last"] = {}
            ssq["guard"] = guard

        def ssq_flush(keep):
            while len(ssq["pend"]) > keep:
                ssq["pend"].pop(0)()

        def ssq_update(c, t0, t1, ready, is_last):
            tgt = 1 if (c % 2 == 0 or c == KC - 1) else 0
            if ssq["dve_only"]:
                tgt = 0
            key = (tgt, t0, t1)
            n = t1 - t0
            acc = statAB[tgt]
            if key not in ssq["first"]:
                tok = act(P, acc[:, t0:t1], h[:, c, t0:t1], AF.Square, waits=[ready, ssq["guard"]])
                ssq["first"][key] = True
                ssq["last"][key] = tok
                return
            i = ssq["n"][tgt] % NSQ[tgt]
            ssq["n"][tgt] += 1
            buf = sqTT[tgt][i]
            a_tok = act(P, buf[:, 0:n], h[:, c, t0:t1], AF.Square, waits=[ready, ssq["rd"].get((tgt, i)), ssq["guard"]])

            def do_add():
                d_tok = tt(P, ACC_ENG[tgt], acc[:, t0:t1], acc[:, t0:t1], buf[:, 0:n], ALU.add,
                           waits=[a_tok, ssq["last"][key]])
                ssq["rd"][(tgt, i)] = d_tok
                ssq["last"][key] = d_tok

            if tgt == 0:
                ssq["pend"].append(do_add)
                ssq_flush(2)
            else:
                do_add()


        xp_ctr = [0]

        def xn_prod(c, t0, t1, gidx, ready, guard, act_share):
            gcol = g_sb[:, gidx * KC + c:gidx * KC + c + 1]
            xp_ctr[0] += 1
            if act_share and xp_ctr[0] % act_share == 0:
                return act(P, xn[:, c, t0:t1], h[:, c, t0:t1], AF.Copy, scale=gcol, waits=[ready, guard, par_tok])
            return ts(P, "dve", xn[:, c, t0:t1], h[:, c, t0:t1], gcol, None, ALU.mult, waits=[ready, guard, par_tok])

        def rstd_finish(tiles, dim):
            out = {}
            ssq_flush(0)
            for (t0, t1) in tiles:
                n = t1 - t0
                b = next_acc()
                parts = [tg for tg in (0, 1) if (tg, t0, t1) in ssq["last"]]
                m_tok = None
                for pi, tg in enumerate(parts):
                    w = [ssq["last"][(tg, t0, t1)], c_ones]
                    if pi == 0:
                        w.append(bank_free[b])
                    m_tok = mm(P, ps[b][:, 0:n], ones[:], statAB[tg][:, t0:t1], pi == 0, pi == len(parts) - 1,
                               sig=True, waits=w)
                s_tok = act(P, statB[:, t0:t1], ps[b][:, 0:n], AF.Sqrt, scale=1.0 / dim, bias=epsb[:, 0:1],
                            waits=[m_tok, c_eps])
                bank_free[b] = s_tok
                out[(t0, t1)] = P.op("dve", lambda hh, o=statB[:, t0:t1]: hh.reciprocal(out=o, in_=o), sig=True,
                                     waits=[s_tok])
            return out

        gu_last = [None]

        def ffn(tag, tiles, xn_tok, rstd_tok, wnames, final_cb, xw_guard, prod_cb=None, rstd_hook=None,
                tile_first=False, pre_tile=None):
            wg, wu, wd = wnames
            hid_ready = {}
            hid_rd = [None, None]
            last_evac = {}
            pe_last = [None]
            gu_last[0] = None

            def GU_unit(g, jj, req, ti, tl, first_pair):
                b = g % 2
                slot = b * GJ + jj
                (gi, gs, gtok), (ui, us, utok) = req
                t0, t1 = tl
                n = t1 - t0
                pr = next_pair()
                bg, bu = ps[2 * pr], ps[2 * pr + 1]
                toks = []
                for (s, wtok, bank, bidx) in ((gs, gtok, bg, 2 * pr), (us, utok, bu, 2 * pr + 1)):
                    for kc in range(KC):
                        w = [xn_tok[(kc, tl)]]
                        if kc == 0:
                            w += [wtok, bank_free[bidx]]
                        tk = mm(P, bank[:, 0:n], ring[:, s, kc * 128:(kc + 1) * 128], xn[:, kc, t0:t1],
                                kc == 0, kc == KC - 1, sig=(kc == KC - 1), waits=w)
                    toks.append(tk)
                if ti == len(tiles) - 1:
                    w_release(gi, toks[0])
                    w_release(ui, toks[1])
                if rstd_hook is not None and g == 0 and jj == 0:
                    rstd_hook(tl)
                rr = statB[:, t0:t1]
                g_tok = tt(P, "dve", stmp[:, pr, 0:n], bg[:, 0:n], rr, ALU.mult,
                           waits=[toks[0], bank_free[2 * pr + 1], rstd_tok[tl]])
                bank_free[2 * pr] = g_tok
                a_tok = act(P, stmp[:, pr, 0:n], stmp[:, pr, 0:n], AF.Silu, waits=[g_tok])
                s_tok2 = tt(P, "dve", stmp[:, pr, 0:n], stmp[:, pr, 0:n], rr, ALU.mult, waits=[a_tok])
                d_tok = tt(P, "dve", hid[:, slot, t0:t1], bu[:, 0:n], stmp[:, pr, 0:n], ALU.mult,
                           waits=[s_tok2, toks[1], hid_rd[b]])
                bank_free[2 * pr + 1] = d_tok
                hid_ready[(slot, tl)] = d_tok
                pe_last[0] = toks[1]

            def GU(g):
                jj0 = 0
                if g == 0 and tile_first:
                    reqs = []
                    for jj in range(2):
                        j = g * GJ + jj
                        reqs.append((w_request(("col", wg, j)), w_request(("col", wu, j))))
                    for ti, tl in enumerate(tiles):
                        if pre_tile is not None:
                            pre_tile(tl)
                        for jj in range(2):
                            GU_unit(g, jj, reqs[jj], ti, tl, True)
                    jj0 = 2
                for jj in range(jj0, GJ):
                    j = g * GJ + jj
                    req = (w_request(("col", wg, j)), w_request(("col", wu, j)))
                    for ti, tl in enumerate(tiles):
                        GU_unit(g, jj, req, ti, tl, False)

            def DN(g, final):
                b = g % 2
                tk = None
                for mq in range(4):
                    di, dslot, dtok = w_request(("down", wd, g, mq))
                    tk = DN_mq(g, dslot, dtok, mq)
                    w_release(di, tk)
                hid_rd[b] = tk
                pe_last[0] = tk

            def DN_last2(ga, gb):
                tk = None
                for mq in range(4):
                    reqs = [w_request(("down", wd, g_, mq)) for g_ in (ga, gb)]
                    for mm_ in range(4):
                        m = mq * 4 + mm_
                        for tl in tiles:
                            t0, t1 = tl
                            n = t1 - t0
                            bk = next_acc()
                            idx = 0
                            for (di, dslot, dtok), g_ in zip(reqs, (ga, gb)):
                                for jj in range(GJ):
                                    slot = (g_ % 2) * GJ + jj
                                    w = [hid_ready[(slot, tl)]]
                                    if jj == 0:
                                        w.append(dtok)
                                    if idx == 0:
                                        w.append(bank_free[bk])
                                    tk = mm(P, ps[bk][:, 0:n],
                                            ring[:, dslot, jj * 512 + mm_ * 128: jj * 512 + (mm_ + 1) * 128],
                                            hid[:, slot, t0:t1], idx == 0, idx == 2 * GJ - 1, sig=(idx == 2 * GJ - 1),
                                            waits=w)
                                    idx += 1
                            e_tok = stt(P, h[:, m, t0:t1], ps[bk][:, 0:n], 0.5, h[:, m, t0:t1], ALU.mult, ALU.add,
                                        waits=[tk, last_evac.get((m, tl))])
                            bank_free[bk] = e_tok
                            last_evac[(m, tl)] = e_tok
                            ssq_update(m, t0, t1, e_tok, m == KC - 1)
                            if prod_cb is not None:
                                prod_cb(m, tl, e_tok)
                    for (di, dslot, dtok) in reqs:
                        w_release(di, tk)
                hid_rd[0] = hid_rd[1] = tk
                pe_last[0] = tk

            def DN_mq(g, dslot, dtok, mq):
                b = g % 2
                tk = None
                for mm_ in range(4):
                    m = mq * 4 + mm_
                    for (t0, t1) in tiles:
                        n = t1 - t0
                        bk = next_acc()
                        for jj in range(GJ):
                            slot = b * GJ + jj
                            w = [hid_ready[(slot, (t0, t1))]]
                            if jj == 0:
                                w += [dtok, bank_free[bk]]
                            tk = mm(P, ps[bk][:, 0:n], ring[:, dslot, jj * 512 + mm_ * 128: jj * 512 + (mm_ + 1) * 128],
                                    hid[:, slot, t0:t1], jj == 0, jj == GJ - 1, sig=(jj == GJ - 1), waits=w)
                        e_tok = stt(P, h[:, m, t0:t1], ps[bk][:, 0:n], 0.5, h[:, m, t0:t1], ALU.mult, ALU.add,
                                    waits=[tk, last_evac.get((m, (t0, t1)))])
                        bank_free[bk] = e_tok
                        last_evac[(m, (t0, t1))] = e_tok
                return tk

            assert NG >= 2
            GU(0)
            for g in range(1, NG):
                GU(g)
                if g < NG - 1:
                    DN(g - 1, False)
            gu_last[0] = pe_last[0]
            DN_last2(NG - 2, NG - 1)
            if final_cb is not None:
                final_cb()
            return pe_last[0], last_evac

        xn_tok, rstd1 = {}, {}
        dgv = [dg[:, i, :, :].rearrange("p a b -> p (a b)") for i in range(2)]
        sq1_rd = [None, None]
        sq1d_rd = [None, None, None]
        bank1 = {tl: 4 + i for i, tl in enumerate(T1)}
        assert len(T1) <= 3
        HPE = KC // 2

        def rstd1_pre(tl):
            t0, t1 = tl
            n = t1 - t0
            bk = bank1[tl]
            for c in range(KC):
                xn_tok[(c, tl)] = xn_prod(c, t0, t1, 0, x_tok[tl], None, 0)
            m_tok = None
            for c in range(HPE):
                i = c % 2
                a_tok = act(P, dgv[i][:, 0:n].bitcast(F32R), h[:, c, t0:t1], AF.Square, waits=[x_tok[tl], sq1_rd[i]])
                w = [a_tok, c_ones]
                if c == 0:
                    w.append(bank_free[bk])
                m_tok = mm(P, ps[bk][:, 0:n], ones_r[:].bitcast(F32R), dgv[i][:, 0:n].bitcast(F32R), c == 0, False,
                           sig=True, waits=w)
                sq1_rd[i] = m_tok
            acc = statAB[0]
            d_tok = act(P, acc[:, t0:t1], h[:, HPE, t0:t1], AF.Square, waits=[x_tok[tl]])
            for c in range(HPE + 1, KC):
                i = c % 3
                buf = sqTT[0][i]
                a_tok = act(P, buf[:, 0:n], h[:, c, t0:t1], AF.Square, waits=[x_tok[tl], sq1d_rd[i]])
                d_tok = tt(P, "dve", acc[:, t0:t1], acc[:, t0:t1], buf[:, 0:n], ALU.add, waits=[a_tok, d_tok])
                sq1d_rd[i] = d_tok
            pend1[tl] = (d_tok, m_tok)

        pend1 = {}

        def rstd1_post(tl):
            t0, t1 = tl
            n = t1 - t0
            bk = bank1[tl]
            d_tok, m_tok = pend1.pop(tl)
            m_tok = mm(P, ps[bk][:, 0:n], ones[:], statAB[0][:, t0:t1], False, True, sig=True,
                       waits=[d_tok, m_tok, c_ones])
            s_tok = act(P, statB[:, t0:t1], ps[bk][:, 0:n], AF.Sqrt, scale=1.0 / D, bias=epsb[:, 0:1],
                        waits=[m_tok, c_eps])
            bank_free[bk] = s_tok
            rstd1[tl] = P.op("dve", lambda hh, o=statB[:, t0:t1]: hh.reciprocal(out=o, in_=o), sig=True, waits=[s_tok])

        ssq_reset(T1, None)
        xn_mix, rstd_mix = {}, {}

        def mix_prod(m, tl, e_tok):
            xn_mix[(m, tl)] = xn_prod(m, tl[0], tl[1], 1, e_tok, gu_last[0], 3)

        pe_tok, evac1 = ffn("f1", T1, xn_tok, rstd1, ("ffn1_w_gate", "ffn1_w_up", "ffn1_w_down"), None, None,
                            mix_prod, rstd1_post, True, rstd1_pre)
        ssq_flush(0)
        win_hooks = [(lambda tl=tl: rstd_mix.update(rstd_finish([tl], D))) for tl in T1]
        xn_tok = xn_mix

        hist_p = DmaSem(P, "hist_p")
        out_p = DmaSem(P, "out_p")
        hist_c = [DmaSem(P, "hist_c0"), DmaSem(P, "hist_c1")]
        out_c = [DmaSem(P, "out_c0"), DmaSem(P, "out_c1")]
        ysem = DmaSem(P, "ysem")

        def win_unit(specs, evac):
            reqs = [w_request(s) for s in specs]
            last = None
            for ti, (t0, t1) in enumerate(T1):
                n = t1 - t0
                pr = next_pair()
                banks = [2 * pr + i for i in range(len(reqs))]
                toks = []
                for (wi, s, wtok), bidx in zip(reqs, banks):
                    for kc in range(KC):
                        w = [xn_tok[(kc, (t0, t1))]]
                        if kc == 0:
                            w += [wtok, bank_free[bidx]]
                        tk = mm(P, ps[bidx][:, 0:n], ring[:, s, kc * 128:(kc + 1) * 128], xn[:, kc, t0:t1],
                                kc == 0, kc == KC - 1, sig=(kc == KC - 1), waits=w)
                    toks.append(tk)
                if ti == len(T1) - 1:
                    for (wi, s, wtok), tk in zip(reqs, toks):
                        w_release(wi, tk)
                if win_hooks:
                    win_hooks.pop(0)()
                evac(ti, t0, t1, banks, toks, pr)
                last = toks[-1]
            return last

        mixn = [v for v in xn_tok.values()]
        upe_free = [mixn, mixn]
        sab_rd = [mixn, mixn]
        pm_rd = [None]
        pm_ready = {}
        pend_pw = []
        cvo_guard = []
        p_tok = {}
        hist_pb = [hist_p, DmaSem(P, "hist_p1")]
        out_pb = [out_p, DmaSem(P, "out_p1")]

        for q in range(PCH):
            g = q // 2
            w_ = POOL_W[g]
            ub = q % 2
            upe = upes[ub]
            h_tok = hist_pb[ub].dma("sp", v3(upe[:, PW:UP])[:, :, 0:PHIST], v3(spT[q]), waits=[upe_free[ub]])
            ev = []

            def evac_pool(ti, t0, t1, banks, toks, pr, ev=ev, upe=upe, ub=ub):
                b = banks[0]
                pe = min(t1, PW)
                rt = rstd_mix[(t0, t1)]
                tk = tt(P, "dve", upe[:, t0:pe], ps[b][:, 0:pe - t0], statB[:, t0:pe], ALU.mult,
                        waits=[toks[0], upe_free[ub], rt])
                if t1 > PW:
                    tk = tt(P, "dve", v3(upe[:, PW:UP])[:, :, PHIST:LP], v3(ps[b][:, PW - t0:PW - t0 + NS]),
                            v3(statB[:, PW:N1]), ALU.mult, waits=[toks[0], rt])
                bank_free[b] = tk
                ev.append(tk)

            win_unit([("col", "w_in", 2 * CCH + q)], evac_pool)
            while pend_pw:
                pend_pw.pop(0)()
            ready = [ev, h_tok]
            o1 = out_pb[ub].dma("sp", o_npp[q], upe[:, PW - PHIST:PW], waits=ready)
            o2 = out_pb[ub].dma("sp", v3(o_nps[q]), v3(upe[:, PW:UP])[:, :, LP - PHIST:LP], waits=ready)
            src, sh, k_tok = upe, 1, ready
            bufs = [S_a, S_b]
            bi = 0
            while sh < w_:
                dst = bufs[bi]
                lo = 2 * sh - 1
                k_tok = tt(P, "dve", dst[:, lo:UP], src[:, lo:UP], src[:, lo - sh:UP - sh], ALU.add,
                           waits=[k_tok, sab_rd[bi]])
                src = dst
                sh *= 2
                bi ^= 1
            kcq = q % 2
            sab_last = bi ^ 1
            w1 = [k_tok, pm_rd[0]]
            t_a = stt(P, pm[:, kcq, 0:cfg.nown], src[:, HALO:PW], 1.0 / w_, upe[:, HALO:PW], ALU.mult, ALU.subtract, waits=w1)
            t_b = stt(P, v3(pm[:, kcq, cfg.nown:N2]), v3(src[:, PW:UP])[:, :, PHIST:LP], 1.0 / w_,
                      v3(upe[:, PW:UP])[:, :, PHIST:LP], ALU.mult, ALU.subtract, waits=w1)
            t_c = tt(P, "dve", tiny[:, 0:16], src[:, HALO:HALO + 16], ic_sb[:, g * 16:(g + 1) * 16], ALU.mult,
                     waits=[k_tok, par_tok])
            t_d = tt(P, "dve", pm[:, kcq, 0:16], tiny[:, 0:16], upe[:, HALO:HALO + 16], ALU.subtract, waits=[t_c, t_a])
            sab_rd[sab_last] = [t_a, t_b, t_d]
            upe_free[ub] = [t_a, t_b, t_d, o2]
            pm_ready[(g, kcq)] = [t_a, t_b, t_d]
            cvo_guard = [upe_free[0], upe_free[1]]
            if kcq == 1:
                def pool_w_mm(g=g):
                    wi, s_, wtok = w_request(("poolw", g))
                    tk = None
                    for dm in range(2):
                        qq = 2 * g + dm
                        for (a, b_), (t0, t1) in zip(C2, T2):
                            n = b_ - a
                            bk = next_acc()
                            for kc in range(2):
                                w = [pm_ready[(g, kc)]]
                                if kc == 0:
                                    w += [wtok, bank_free[bk]]
                                tk = mm(P, ps[bk][:, 0:n], ring[:, s_, kc * 256 + dm * 128: kc * 256 + (dm + 1) * 128],
                                        pm[:, kc, a:b_], kc == 0, kc == 1, sig=(kc == 1), waits=w)
                            e_tok = act(P, hid[:, qq, t0:t1], ps[bk][:, 0:n], AF.Copy,
                                        scale=cv_sb[:, 2 * CCH + qq:2 * CCH + qq + 1], waits=[tk, par_tok, pe_tok])
                            bank_free[bk] = e_tok
                            p_tok[(qq, (t0, t1))] = e_tok
                    w_release(wi, tk)
                    pm_rd[0] = tk
                pend_pw.append(pool_w_mm)

        uce_free = [None, None]
        assert CV % 2 == 0
        E_T = [(2 * a, 2 * b) for a, b in _split(0, CV // 2, 256)]
        conv_banks = [4, 5, 6]
        assert len(E_T) <= 3 and E_T[-1][0] <= cfg.nown
        dg_rd = [None, None]
        TG = 4
        conv_in = {}
        sq_rd = [None]
        cvk = lambda k: cvo[:, k * N2:(k + 1) * N2]
        cvo_done = {}
        ssq_last = [None]

        def conv_win(k):
            bf = k % 2
            ue = uce[:, bf, :]
            h_tok = hist_c[bf].dma("sp", v3(ue[:, PW:UC].bitcast(F32R))[:, :, 0:HALO], v3(scT[k].bitcast(F32R)),
                                   waits=[uce_free[bf]])
            ev = []

            def evac_conv(ti, t0, t1, banks, toks, pr):
                ba, bg = banks
                n = t1 - t0
                rr = statB[:, t0:t1]
                g_tok = tt(P, "dve", stmp[:, pr, 0:n], ps[bg][:, 0:n], rr, ALU.mult,
                           waits=[toks[1], bank_free[ba], rstd_mix[(t0, t1)]])
                bank_free[bg] = g_tok
                a_tok0 = act(P, stmp[:, pr, 0:n], stmp[:, pr, 0:n], AF.Sigmoid, waits=[g_tok])
                a_tok = tt(P, "dve", stmp[:, pr, 0:n], stmp[:, pr, 0:n], rr, ALU.mult, waits=[a_tok0])
                pe = min(t1, PW)
                tk = tt(P, "dve", ue[:, t0:pe].bitcast(F32R), ps[ba][:, 0:pe - t0], stmp[:, pr, 0:pe - t0], ALU.mult,
                        waits=[a_tok, toks[0], uce_free[bf]])
                if t1 > PW:
                    o = PW - t0
                    tk = tt(P, "dve", v3(ue[:, PW:UC].bitcast(F32R))[:, :, HALO:LC], v3(ps[ba][:, o:o + NS]),
                            v3(stmp[:, pr, o:o + NS]), ALU.mult, waits=[a_tok, toks[0]])
                bank_free[ba] = tk
                ev.append(tk)

            pe_win = win_unit([("col", "w_in", k), ("col", "w_in", CCH + k)], evac_conv)
            ready = [ev, h_tok]
            o1 = out_c[bf].dma("sp", o_ncp[k], ue[:, PW - HALO:PW], waits=ready)
            o2 = out_c[bf].dma("sp", v3(o_ncs[k]), v3(ue[:, PW:UC])[:, :, LC - HALO:LC], waits=ready)
            conv_in[k] = (ready, [o2], pe_win)

        NGRP = -(-CONV_K // TG)
        assert NGRP % 2 == 0
        prebuilt = {}

        def diag_build(k, gi):
            b = gi % 2
            taps = list(range(gi * TG, min(gi * TG + TG, CONV_K)))
            bt = None
            for jj, i in enumerate(taps):
                bt = act(P, dg[:, b, jj, :].bitcast(F32R), ident_sb[:], AF.Copy,
                         scale=cw_sb[:, k * CONV_K + i:k * CONV_K + i + 1], waits=[dg_rd[b], par_tok])
            return bt

        def conv_prebuild(k):
            for gi in range(2):
                prebuilt[(k, gi)] = diag_build(k, gi)

        def conv_pe(k):
            bf = k % 2
            ue = uce[:, bf, :]
            ready, outs, _ = conv_in[k]
            tk = None
            for gi, j0 in enumerate(range(0, CONV_K, TG)):
                taps = list(range(j0, min(j0 + TG, CONV_K)))
                b = gi % 2
                bt = prebuilt.pop((k, gi), None)
                if bt is None:
                    bt = diag_build(k, gi)
                for jj, i in enumerate(taps):
                    for (e0, e1), bk in zip(E_T, conv_banks):
                        w = [bt]
                        if i == 0:
                            w += [ready, bank_free[bk]]
                        lastt = (jj == len(taps) - 1 and (e0, e1) == E_T[-1])
                        tk = mm(P, ps[bk][:, 0:e1 - e0], dg[:, b, jj, :].bitcast(F32R),
                                ue[:, e0 + i:e1 + i].bitcast(F32R), i == 0, i == CONV_K - 1, sig=lastt, waits=w)
                dg_rd[b] = tk
            uce_free[bf] = [tk, outs]
            ck = cvk(k)
            bcol = cv_sb[:, k:k + 1]
            ev = []
            for (e0, e1), bk in zip(E_T, conv_banks):
                pe = min(e1, cfg.nown)
                t_ = None
                if e0 < cfg.nown:
                    t_ = ts(P, "dve", ck[:, e0:pe], ps[bk][:, 0:pe - e0], bcol, None, ALU.add,
                            waits=[tk, par_tok, cvo_guard])
                if e1 == CV:
                    o = cfg.nown - e0
                    t_ = ts(P, "dve", v3(ck[:, cfg.nown:N2]), v3(ps[bk][:, o:o + NSTR * LC])[:, :, HALO:LC], bcol, None,
                            ALU.add, waits=[tk, par_tok, cvo_guard])
                bank_free[bk] = t_
                ev.append(t_)
            cvo_done[k] = ev
            if k == 0:
                ssq_last[0] = act(P, ssqc[:, 0:N2], ck[:, 0:N2], AF.Square, waits=[ev])
            else:
                s1 = act(P, pmf[:, 0:N2], ck[:, 0:N2], AF.Square, waits=[ev, sq_rd[0], pm_rd[0]])
                ssq_last[0] = tt(P, "dve", ssqc[:, 0:N2], ssqc[:, 0:N2], pmf[:, 0:N2], ALU.add, waits=[s1, ssq_last[0]])
                sq_rd[0] = ssq_last[0]

        conv_win(0)
        while pend_pw:
            pend_pw.pop(0)()
        for k in range(CCH):
            conv_prebuild(k)
            if k + 1 < CCH:
                conv_win(k + 1)
            conv_pe(k)
        pe_win_last = conv_in[CCH - 1][2]

        pe_conv_last = uce_free[(CCH - 1) % 2][0]
        evac_wo = {}
        pe_wo = [None]

        def wout_group(half, mp, s_, wtok, tl, with_ssq):
            t0, t1 = tl
            n = t1 - t0
            tk = None
            for mm_ in range(2):
                m = mp * 2 + mm_
                bk = next_acc()
                for kc in range(CCH):
                    if half == 0:
                        rhs, w = xn[:, kc, t0:t1], [c_tok[(kc, tl)]]
                    else:
                        rhs, w = hid[:, kc, t0:t1], [p_tok[(kc, tl)]]
                    if kc == 0:
                        w += [wtok, bank_free[bk]]
                    tk = mm(P, ps[bk][:, 0:n], ring[:, s_, kc * 256 + mm_ * 128: kc * 256 + (mm_ + 1) * 128], rhs,
                            kc == 0, kc == CCH - 1, sig=(kc == CCH - 1), waits=w)
                e_tok = tt(P, "dve", h[:, m, t0:t1], ps[bk][:, 0:n], h[:, m, t0:t1], ALU.add,
                           waits=[tk, evac_wo.get((m, tl))] + [evac1[k_] for k_ in evac1 if k_[0] == m])
                bank_free[bk] = e_tok
                evac_wo[(m, tl)] = e_tok
                if with_ssq:
                    ssq_update(m, t0, t1, e_tok, m == KC - 1)
                    if m >= CCH:
                        xn_f2[(m, tl)] = xn_prod(m, t0, t1, 2, e_tok, None, 3)
                    else:
                        xn_late.append((m, tl, e_tok))
            return tk

        xn_f2 = {}
        xn_late = []

        def wout_pass(half, mps, with_ssq=False, after_group=None):
            for mp in mps:
                wi, s_, wtok = w_request(("wout", half, mp))
                tk = None
                for tl in T2:
                    tk = wout_group(half, mp, s_, wtok, tl, with_ssq)
                    if after_group is not None:
                        after_group()
                w_release(wi, tk)
                pe_wo[0] = tk

        wout_pass(1, range(0, 1))

        c_rstd = []
        chain = []
        for (a, b_) in C2:
            bk = next_acc()
            m_tok = mm(P, ps[bk][:, 0:b_ - a], ones[:], ssqc[:, a:b_], True, True, sig=True,
                       waits=[ssq_last[0], c_ones, bank_free[bk]])
            s_tok = act(P, statB[:, a:b_], ps[bk][:, 0:b_ - a], AF.Sqrt, scale=1.0 / (CCH * 128), bias=epsb[:, 0:1],
                        waits=[m_tok, c_eps, ssq_last[0]])
            bank_free[bk] = s_tok

            def recip(a=a, b_=b_, s_tok=s_tok):
                c_rstd.append(P.op("dve", lambda hh, o=statB[:, a:b_]: hh.reciprocal(out=o, in_=o), sig=True,
                                   waits=[s_tok]))
            chain.append(recip)
        c_tok = {}

        def make_c(k):
            gcol = cv_sb[:, CCH + k:CCH + k + 1]
            ck = cvk(k)
            n1 = stt(P, ck[:, 0:N2], ck[:, 0:N2], gcol, statB[:, 0:N2], ALU.mult, ALU.mult, waits=[c_rstd, par_tok])
            a1 = act(P, xn[:, k, HALO:N1], ck[:, 0:N2], AF.Silu, waits=[n1, pe_win_last])
            for tl in T2:
                c_tok[(k, tl)] = [a1]
        for k in range(CCH):
            chain.append(lambda k=k: make_c(k))

        wout_pass(1, range(1, KC // 2), False, lambda: chain.pop(0)() if chain else None)
        while chain:
            chain.pop(0)()
        ssq_reset(T2, [pe_conv_last, [v for v in c_tok.values()]])
        wout_pass(0, range(0, KC // 2), True)
        for (m_, tl_, e_) in xn_late:
            xn_f2[(m_, tl_)] = xn_prod(m_, tl_[0], tl_[1], 2, e_, pe_wo[0], 3)
        ssq_flush(0)
        ssq_f2 = dict(ssq["last"])
        rstd_f2 = {}

        def rstd2_hook(tl):
            keep = ssq["last"]
            ssq["last"] = dict(ssq_f2)
            rstd_f2.update(rstd_finish([tl], D))
            ssq["last"] = keep


        ssq_reset(T2, None)
        yTv = yT.rearrange("c p t -> p c t")
        y_last = [None]

        fin_scaled = {}

        def fin_prod(m, tl, e_tok):
            fin_scaled[(m, tl)] = act(P, h[:, m, tl[0]:tl[1]], h[:, m, tl[0]:tl[1]], AF.Copy,
                                      scale=g_sb[:, 3 * KC + m:3 * KC + m + 1], waits=[e_tok, par_tok])

        def fin_cb():
            half = KC // 2
            for tl in T2:
                r_tok = rstd_finish([tl], D)[tl]
                toks = []
                for c in range(KC):
                    toks.append(tt(P, "dve", h[:, c, tl[0]:tl[1]], h[:, c, tl[0]:tl[1]], statB[:, tl[0]:tl[1]], ALU.mult,
                                   waits=[r_tok, fin_scaled[(c, tl)]]))
                    if c == half - 1 or c == KC - 1:
                        c0 = 0 if c == half - 1 else half
                        y_last[0] = ysem.dma("sp", yTv[:, c0:c + 1, tl[0] - HALO:tl[1] - HALO],
                                             h[:, c0:c + 1, tl[0]:tl[1]], waits=toks[c0:c + 1])

        pe_tok2, evac2 = ffn("f2", T2, xn_f2, rstd_f2, ("ffn2_w_gate", "ffn2_w_up", "ffn2_w_down"), fin_cb,
                             [v for v in c_tok.values()], fin_prod, rstd2_hook)
        last = y_last[0]
        P.wait("sp", last)
        for dsm in (out_pb[0], out_pb[1], out_c[0], out_c[1]):
            P.wait("sp", Tok(dsm.sem, 16 * dsm.n, ("dma", id(dsm.sem))))
        assert len(plan) == n_w, (len(plan), n_w)
        P.run(block)
    return nc, plan


def _col_tile(W, m):
    blk = W[:, m * 128:(m + 1) * 128].reshape(KC, 128, 128)
    return np.ascontiguousarray(blk.transpose(1, 0, 2)).reshape(128, 2048)


def _down_tile(Wd, g, mq):
    blk = Wd[g * GJ * 128:(g + 1) * GJ * 128, mq * 512:(mq + 1) * 512].reshape(GJ, 128, 512)
    return np.ascontiguousarray(blk.transpose(1, 0, 2)).reshape(128, 2048)


def _wout_tile(W, half, mp):
    blk = W[half * 1024:(half + 1) * 1024, mp * 256:(mp + 1) * 256].reshape(CCH, 128, 256)
    return np.ascontiguousarray(blk.transpose(1, 0, 2)).reshape(128, 2048)


def _poolw_tile(pw, g):
    t = np.zeros((128, 2048), np.float32)
    blk = pw[g].reshape(2, 128, 256).transpose(1, 0, 2).reshape(128, 512)
    t[:, :512] = blk
    return t


def _pack_weights(plan, W):
    ws = np.empty((len(plan), 128, 2048), np.float32)
    for i, spec in enumerate(plan):
        if spec[0] == "col":
            ws[i] = _col_tile(W[spec[1]], spec[2])
        elif spec[0] == "down":
            ws[i] = _down_tile(W[spec[1]], spec[2], spec[3])
        elif spec[0] == "wout":
            ws[i] = _wout_tile(W["w_out"], spec[1], spec[2])
        else:
            ws[i] = _poolw_tile(W["pool_w"], spec[1])
    return ws


def _chunks(v, n):
    return np.ascontiguousarray(np.asarray(v, np.float32).reshape(n, 128).T)


_CACHE = {}


def run(inputs, cfg):
    key = (cfg.dff, cfg.nown, tuple(cfg.T1))
    if key not in _CACHE:
        _CACHE[key] = build_program(cfg)
    nc, plan = _CACHE[key]
    f = lambda k: np.asarray(inputs[k], np.float32)
    W = {k: f(k)[0] for k in ("ffn1_w_gate", "ffn1_w_up", "ffn1_w_down", "w_in", "w_out",
                              "ffn2_w_gate", "ffn2_w_up", "ffn2_w_down", "pool_w")}
    ws = _pack_weights(plan, W)
    gains = np.concatenate([_chunks(f("ffn1_norm")[0], KC), _chunks(f("mix_norm")[0], KC),
                            _chunks(f("ffn2_norm")[0], KC), _chunks(f("final_norm"), KC)], axis=1)
    cwv = f("conv_w")[0]
    cw = np.ascontiguousarray(cwv.T.reshape(CCH, 128, CONV_K).transpose(1, 0, 2)).reshape(128, CCH * CONV_K)
    cvec = np.concatenate([_chunks(f("conv_b")[0], CCH), _chunks(f("conv_norm")[0], CCH),
                           _chunks(f("pool_scale")[0], PCH)], axis=1)
    seq = np.concatenate([f("meta_tokens"), f("x_prompt")[0]], axis=0)
    nown = cfg.nown
    assert seq.shape[0] == NCORES * nown
    xs = f("x_sample")
    sc = f("state_conv")[0]
    sp = f("state_pool")[0]
    in_maps = []
    for c in range(NCORES):
        rows = np.zeros((cfg.N1, D), np.float32)
        lo = c * nown - HALO
        if lo >= 0:
            rows[0:cfg.PW] = seq[lo:lo + cfg.PW]
        else:
            rows[HALO:cfg.PW] = seq[0:nown]
        rows[cfg.PW:] = xs[c * NSTR:(c + 1) * NSTR].reshape(NS, D)
        xt3 = rows.T.reshape(KC, 128, cfg.N1).transpose(1, 0, 2)
        xT = np.concatenate([xt3[:, :, a:b].reshape(128, -1) for a, b in cfg.T1], axis=1)
        xT = np.ascontiguousarray(xT)
        icnt = np.empty((128, 4, 16), np.float32)
        for g, w_ in enumerate(POOL_W):
            for i in range(16):
                cnt = min(i + 1, w_) if c == 0 else w_
                icnt[:, g, i] = np.float32(1.0) / np.float32(cnt)
        scT = np.ascontiguousarray(sc[c * NSTR:(c + 1) * NSTR].reshape(NSTR, HALO, CCH, 128).transpose(2, 3, 0, 1))
        spT = np.ascontiguousarray(sp[c * NSTR:(c + 1) * NSTR].reshape(NSTR, PHIST, PCH, 128).transpose(2, 3, 0, 1))
        in_maps.append({
            "xT": xT, "ws": ws, "gains": gains, "cw": cw, "cvec": cvec,
            "icnt": icnt.reshape(128, 64), "ident": np.eye(128, dtype=np.float32),
            "scT": scT.reshape(CCH, 128, NSTR * HALO), "spT": spT.reshape(PCH, 128, NSTR * PHIST),
        })
    res = run_bass_kernel_spmd(nc, in_maps, core_ids=list(range(NCORES)))
    outs = res.results
    yp, ys = [], []
    for c in range(NCORES):
        y = np.asarray(outs[c]["yT"], np.float32).reshape(D, cfg.N2)
        yp.append(y[:, :nown].T)
        ys.append(y[:, nown:].reshape(D, NSTR, DSEQ).transpose(1, 2, 0))
    y_prompt = np.concatenate(yp, axis=0)[N_META:][None]
    y_sample = np.concatenate(ys, axis=0)
    last = outs[NCORES - 1]
    ncp = np.asarray(last["ncp"], np.float32).reshape(CCH * 128, HALO).T[None, None]
    npp = np.asarray(last["npp"], np.float32).reshape(PCH * 128, PHIST).T[None, None]
    ncs = np.concatenate([np.asarray(outs[c]["ncs"], np.float32).reshape(CCH * 128, NSTR, HALO).transpose(1, 2, 0)
                          for c in range(NCORES)], axis=0)[None]
    nps = np.concatenate([np.asarray(outs[c]["nps"], np.float32).reshape(PCH * 128, NSTR, PHIST).transpose(1, 2, 0)
                          for c in range(NCORES)], axis=0)[None]
    return (np.ascontiguousarray(y_prompt), np.ascontiguousarray(y_sample), np.ascontiguousarray(ncp),
            np.ascontiguousarray(npp), np.ascontiguousarray(ncs), np.ascontiguousarray(nps))


def kernel(**inputs):
    return run(inputs, Cfg())
```

```python
from contextlib import ExitStack

import numpy as np

import concourse.bass as bass
import concourse.mybir as mybir
from concourse.bass_utils import run_bass_kernel_spmd

F32 = mybir.dt.float32
BF16 = mybir.dt.bfloat16
F32R = mybir.dt.float32r
AF = mybir.ActivationFunctionType
ALU = mybir.AluOpType

NCORES = 8
D = 2048
KC = D // 128
CCH = 8
PCH = 8
HALO = 30
CONV_K = 31
PHIST = 15
NSTR = 4
DSEQ = 16
NS = NSTR * DSEQ
LC = HALO + DSEQ
LP = PHIST + DSEQ
POOL_W = (2, 4, 8, 16)
N_META = 16
EPS = 1e-6
GJ = 4
R = 4
ENGS = ("pe", "act", "dve", "pool", "sp")


def _split(a, b, maxtile):
    n = b - a
    u = 2 if n % 2 == 0 else 1
    k = -(-n // maxtile)
    base, rem = divmod(n // u, k)
    out, s = [], a
    for i in range(k):
        w = u * (base + (1 if i < rem else 0))
        out.append((s, s + w))
        s += w
    assert s == b
    return out


class Cfg:
    def __init__(self, dff=5632, nown=1026, maxtile=512):
        self.dff = dff
        self.jc = dff // 128
        assert self.jc % GJ == 0
        self.ng = self.jc // GJ
        self.nown = nown
        self.PW = HALO + nown
        self.N1 = self.PW + NS
        self.N2 = nown + NS
        self.UC = self.PW + NSTR * LC
        self.UP = self.PW + NSTR * LP
        self.CV = self.UC - HALO
        self.T1 = _split(0, self.N1, maxtile)
        self.T2 = _split(HALO, self.N1, maxtile)
        self.C2 = [(a - HALO, b - HALO) for a, b in self.T2]
        for tl in (self.T1, self.T2):
            assert tl[-1][0] <= self.PW and all(t[1] <= self.PW for t in tl[:-1])
        assert self.N1 <= 1240 and self.CV >= self.N1


class Tok:
    __slots__ = ("sem", "val", "key")

    def __init__(self, sem, val, key):
        self.sem, self.val, self.key = sem, val, key


class Prog:
    def __init__(self, nc, stack):
        self.nc = nc
        self.stack = stack
        self.q = {e: [] for e in ENGS}
        self.clk = {e: stack.enter_context(nc.semaphore("clk_" + e)) for e in ENGS}
        self.cnt = {e: 0 for e in ENGS}
        self.waited = {e: {} for e in ENGS}
        self.nsem = 0

    def new_sem(self, name):
        self.nsem += 1
        return self.stack.enter_context(self.nc.semaphore(name))

    def wait(self, eng, *toks):
        for t in toks:
            if t is None:
                continue
            if isinstance(t, (list, tuple)):
                self.wait(eng, *t)
                continue
            w = self.waited[eng]
            if w.get(t.key, 0) >= t.val:
                continue
            w[t.key] = t.val
            self.q[eng].append(("wait", t.sem, t.val))

    def op(self, eng, fn, sig=False, waits=()):
        self.wait(eng, *waits)
        if sig:
            self.cnt[eng] += 1
            self.q[eng].append(("op", fn, self.clk[eng]))
            return Tok(self.clk[eng], self.cnt[eng], eng)
        self.q[eng].append(("op", fn, None))
        return None

    def dma(self, eng, out, in_, sem, val, waits=()):
        self.wait(eng, *waits)
        self.q[eng].append(("dma", out, in_, sem))
        return Tok(sem, val, ("dma", id(sem)))

    def run(self, block):
        handles = {"pe": "tensor", "act": "scalar", "dve": "vector", "pool": "gpsimd", "sp": "sync"}

        def make(e):
            items = self.q[e]

            def body(h):
                for it in items:
                    if it[0] == "wait":
                        h.wait_ge(it[1], it[2])
                    elif it[0] == "op":
                        ins = it[1](h)
                        if it[2] is not None:
                            ins.then_inc(it[2], 1)
                    else:
                        h.dma_start(out=it[1], in_=it[2]).then_inc(it[3], 16)
            return body

        for e in ENGS:
            getattr(block, handles[e])(make(e))


class DmaSem:
    def __init__(self, P, name):
        self.P = P
        self.sem = P.new_sem(name)
        self.n = 0

    def dma(self, eng, out, in_, waits=()):
        self.n += 1
        return self.P.dma(eng, out, in_, self.sem, 16 * self.n, waits=waits)


def mm(P, out, lhsT, rhs, start, stop, sig=False, waits=()):
    return P.op("pe", lambda h: h.matmul(out, lhsT, rhs, start=start, stop=stop), sig=sig, waits=waits)


def act(P, out, in_, func, scale=None, bias=None, waits=(), sig=True):
    kw = {}
    if scale is not None:
        kw["scale"] = scale
    if bias is not None:
        kw["bias"] = bias
    return P.op("act", lambda h: h.activation(out=out, in_=in_, func=func, **kw), sig=sig, waits=waits)


def tt(P, eng, out, in0, in1, op, waits=(), sig=True):
    return P.op(eng, lambda h: h.tensor_tensor(out=out, in0=in0, in1=in1, op=op), sig=sig, waits=waits)


def ts(P, eng, out, in0, s1, s2, op0, op1=None, waits=(), sig=True):
    if op1 is None:
        return P.op(eng, lambda h: h.tensor_scalar(out=out, in0=in0, scalar1=s1, scalar2=None, op0=op0),
                    sig=sig, waits=waits)
    return P.op(eng, lambda h: h.tensor_scalar(out=out, in0=in0, scalar1=s1, scalar2=s2, op0=op0, op1=op1),
                sig=sig, waits=waits)


def stt(P, out, in0, scalar, in1, op0, op1, waits=(), sig=True):
    return P.op("dve", lambda h: h.scalar_tensor_tensor(out=out, in0=in0, scalar=scalar, in1=in1, op0=op0, op1=op1),
                sig=sig, waits=waits)


def v3(ap, s=NSTR):
    return ap.rearrange("p (s l) -> p s l", s=s)


def build_program(cfg):
    N1, N2, PW, UC, UP, CV = cfg.N1, cfg.N2, cfg.PW, cfg.UC, cfg.UP, cfg.CV
    T1, T2, C2 = cfg.T1, cfg.T2, cfg.C2
    JC, NG = cfg.jc, cfg.ng
    nc = bass.Bass("TRN2", target_bir_lowering=False)
    plan = []

    n_w = 2 * (3 * JC) + PCH + NSTR + 2 * CCH + KC

    def din(name, shape, dt=F32):
        return nc.dram_tensor(name, list(shape), dt, kind="ExternalInput").ap()

    def dout(name, shape, dt=F32):
        return nc.dram_tensor(name, list(shape), dt, kind="ExternalOutput").ap()

    xT = din("xT", [128, KC * N1])
    ws = din("ws", [n_w, 128, 2048])
    gains = din("gains", [128, 4 * KC])
    cw = din("cw", [128, CCH * CONV_K])
    cvec = din("cvec", [128, 3 * CCH])
    icnt = din("icnt", [128, 4 * 16])
    ident = din("ident", [128, 128])
    scT = din("scT", [CCH, 128, NSTR * HALO])
    spT = din("spT", [PCH, 128, NSTR * PHIST])
    yT = dout("yT", [KC, 128, N2])
    o_ncp = dout("ncp", [CCH, 128, HALO])
    o_npp = dout("npp", [PCH, 128, PHIST])
    o_ncs = dout("ncs", [CCH, 128, NSTR * HALO])
    o_nps = dout("nps", [PCH, 128, NSTR * PHIST])

    with ExitStack() as st:
        P = Prog(nc, st)

        def sb(name, shape, dt=F32):
            return st.enter_context(nc.sbuf_tensor(name, list(shape), dt))

        h = sb("h", [128, KC, N1])
        xn = sb("xn", [128, KC, N1], BF16)
        hid = sb("hid", [128, 2 * GJ, N1], BF16)
        ring = sb("ring", [128, R, 2048], BF16)
        sqw = max(b_ - a_ for a_, b_ in T1 + T2)
        scratch_end = 4 * sqw + 2 * N1
        cvo_cols = max(CCH * N2, scratch_end + 4 * UP)
        cvo = sb("cvo", [128, cvo_cols])
        ssqc = sb("ssqc", [128, N2])
        dg = sb("dg", [128, 2, 4, 128])
        ident_sb = sb("ident_sb", [128, 128])
        uce = sb("uce", [128, 2, UC])
        pm = sb("pm", [128, 2, N2], BF16)
        pmf = pm[:].rearrange("p a b -> p (a b)").bitcast(F32)
        stmp = sb("stmp", [128, 2, 512])
        statB = sb("statB", [128, N1])
        g_sb = sb("g_sb", [128, 4 * KC])
        cw_sb = sb("cw_sb", [128, CCH * CONV_K])
        cv_sb = sb("cv_sb", [128, 3 * CCH])
        ic_sb = sb("ic_sb", [128, 64])
        ones = sb("ones", [128, 128])
        ones_r = sb("ones_r", [128, 128])
        epsb = sb("epsb", [128, 1])
        tiny = sb("tiny", [128, 16])
        ps = [st.enter_context(nc.psum_tensor(f"ps{i}", [128, 512], F32)) for i in range(8)]

        sqTT = [[cvo[:, i * sqw:(i + 1) * sqw] for i in range(3)],
                [cvo[:, 3 * sqw:4 * sqw]]]
        statAB = [cvo[:, 4 * sqw:4 * sqw + N1], cvo[:, 4 * sqw + N1:4 * sqw + 2 * N1]]
        ptb = cvo_cols - 4 * UP
        assert ptb >= scratch_end
        upes = [cvo[:, ptb:ptb + UP], cvo[:, ptb + UP:ptb + 2 * UP]]
        S_a = cvo[:, ptb + 2 * UP:ptb + 3 * UP]
        S_b = cvo[:, ptb + 3 * UP:ptb + 4 * UP]

        block = st.enter_context(nc.Block())

        slot_sem = [P.new_sem(f"slot{i}") for i in range(R)]
        rel = {}
        load_tok = {}

        def w_issue(i, extra=None):
            s = i % R
            waits = [rel[i - R]] if i >= R else []
            if extra is not None:
                waits = waits + [extra]
            load_tok[i] = P.dma("pool", ring[:, s, :], ws[i], slot_sem[s], 16 * (i // R + 1), waits=waits)

        def w_release(i, tok):
            rel[i] = tok
            if i + R < n_w and (i + R) not in load_tok:
                w_issue(i + R)

        def w_request(spec):
            i = len(plan)
            plan.append(spec)
            assert i < n_w and i in load_tok, f"ring too small at tile {i}"
            return i, i % R, load_tok[i]

        bank_free = [None] * 8
        acc_rr = [0]

        def next_acc():
            b = 4 + acc_rr[0] % 4
            acc_rr[0] += 1
            return b

        unit_rr = [0]

        def next_pair():
            p = unit_rr[0] % 2
            unit_rr[0] += 1
            return p

        c_ones0 = P.op("pool", lambda hh: hh.memset(ones[:], 1.0), sig=True)
        c_ones = act(P, ones_r[:].bitcast(F32R), ones[:], AF.Copy, waits=[c_ones0])
        c_eps = P.op("pool", lambda hh: hh.memset(epsb[:], EPS), sig=True)
        for i in range(min(2, n_w)):
            w_issue(i)
        psem = DmaSem(P, "psem")
        for dst, src in ((g_sb, gains), (cw_sb, cw), (cv_sb, cvec), (ic_sb, icnt), (ident_sb, ident)):
            par_tok = psem.dma("sp", dst[:], src)
        x_tok = {}
        for ti, (t0, t1) in enumerate(T1):
            src = xT[:, KC * t0:KC * t1].rearrange("p (c t) -> p c t", c=KC)
            hk = KC // 2
            ta = DmaSem(P, f"xs{ti}a").dma("sp", h[:, 0:hk, t0:t1], src[:, 0:hk, :])
            tb = DmaSem(P, f"xs{ti}b").dma("act", h[:, hk:KC, t0:t1], src[:, hk:KC, :])
            x_tok[(t0, t1)] = [ta, tb]
        for i in range(2, min(R, n_w)):
            w_issue(i, x_tok[T1[0]])

        ssq = {"first": {}, "last": {}, "rd": {}, "n": [0, 0], "guard": None, "pend": [], "dve_only": False}
        ACC_ENG = ("dve", "pool")
        NSQ = (3, 1)

        def ssq_reset(tiles, guard):
            assert not ssq["pend"]
            ssq["first"] = {}
            ssq["last"] = {}
            ssq["guard"] = guard

        def ssq_flush(keep):
            while len(ssq["pend"]) > keep:
                ssq["pend"].pop(0)()

        def ssq_update(c, t0, t1, ready, is_last):
            tgt = 1 if (c % 2 == 0 or c == KC - 1) else 0
            if ssq["dve_only"]:
                tgt = 0
            key = (tgt, t0, t1)
            n = t1 - t0
            acc = statAB[tgt]
            if key not in ssq["first"]:
                tok = act(P, acc[:, t0:t1], h[:, c, t0:t1], AF.Square, waits=[ready, ssq["guard"]])
                ssq["first"][key] = True
                ssq["last"][key] = tok
                return
            i = ssq["n"][tgt] % NSQ[tgt]
            ssq["n"][tgt] += 1
            buf = sqTT[tgt][i]
            a_tok = act(P, buf[:, 0:n], h[:, c, t0:t1], AF.Square, waits=[ready, ssq["rd"].get((tgt, i)), ssq["guard"]])

            def do_add():
                d_tok = tt(P, ACC_ENG[tgt], acc[:, t0:t1], acc[:, t0:t1], buf[:, 0:n], ALU.add,
                           waits=[a_tok, ssq["last"][key]])
                ssq["rd"][(tgt, i)] = d_tok
                ssq["last"][key] = d_tok

            if tgt == 0:
                ssq["pend"].append(do_add)
                ssq_flush(2)
            else:
                do_add()


        xp_ctr = [0]

        def xn_prod(c, t0, t1, gidx, ready, guard, act_share):
            gcol = g_sb[:, gidx * KC + c:gidx * KC + c + 1]
            xp_ctr[0] += 1
            if act_share and xp_ctr[0] % act_share == 0:
                return act(P, xn[:, c, t0:t1], h[:, c, t0:t1], AF.Copy, scale=gcol, waits=[ready, guard, par_tok])
            return ts(P, "dve", xn[:, c, t0:t1], h[:, c, t0:t1], gcol, None, ALU.mult, waits=[ready, guard, par_tok])

        def rstd_finish(tiles, dim):
            out = {}
            ssq_flush(0)
            for (t0, t1) in tiles:
                n = t1 - t0
                b = next_acc()
                parts = [tg for tg in (0, 1) if (tg, t0, t1) in ssq["last"]]
                m_tok = None
                for pi, tg in enumerate(parts):
                    w = [ssq["last"][(tg, t0, t1)], c_ones]
                    if pi == 0:
                        w.append(bank_free[b])
                    m_tok = mm(P, ps[b][:, 0:n], ones[:], statAB[tg][:, t0:t1], pi == 0, pi == len(parts) - 1,
                               sig=True, waits=w)
                s_tok = act(P, statB[:, t0:t1], ps[b][:, 0:n], AF.Sqrt, scale=1.0 / dim, bias=epsb[:, 0:1],
                            waits=[m_tok, c_eps])
                bank_free[b] = s_tok
                out[(t0, t1)] = P.op("dve", lambda hh, o=statB[:, t0:t1]: hh.reciprocal(out=o, in_=o), sig=True,
                                     waits=[s_tok])
            return out

        gu_last = [None]

        def ffn(tag, tiles, xn_tok, rstd_tok, wnames, final_cb, xw_guard, prod_cb=None, rstd_hook=None,
                tile_first=False, pre_tile=None):
            wg, wu, wd = wnames
            hid_ready = {}
            hid_rd = [None, None]
            last_evac = {}
            pe_last = [None]
            gu_last[0] = None

            def GU_unit(g, jj, req, ti, tl, first_pair):
                b = g % 2
                slot = b * GJ + jj
                (gi, gs, gtok), (ui, us, utok) = req
                t0, t1 = tl
                n = t1 - t0
                pr = next_pair()
                bg, bu = ps[2 * pr], ps[2 * pr + 1]
                toks = []
                for (s, wtok, bank, bidx) in ((gs, gtok, bg, 2 * pr), (us, utok, bu, 2 * pr + 1)):
                    for kc in range(KC):
                        w = [xn_tok[(kc, tl)]]
                        if kc == 0:
                            w += [wtok, bank_free[bidx]]
                        tk = mm(P, bank[:, 0:n], ring[:, s, kc * 128:(kc + 1) * 128], xn[:, kc, t0:t1],
                                kc == 0, kc == KC - 1, sig=(kc == KC - 1), waits=w)
                    toks.append(tk)
                if ti == len(tiles) - 1:
                    w_release(gi, toks[0])
                    w_release(ui, toks[1])
                if rstd_hook is not None and g == 0 and jj == 0:
                    rstd_hook(tl)
                rr = statB[:, t0:t1]
                g_tok = tt(P, "dve", stmp[:, pr, 0:n], bg[:, 0:n], rr, ALU.mult,
                           waits=[toks[0], bank_free[2 * pr + 1], rstd_tok[tl]])
                bank_free[2 * pr] = g_tok
                a_tok = act(P, stmp[:, pr, 0:n], stmp[:, pr, 0:n], AF.Silu, waits=[g_tok])
                s_tok2 = tt(P, "dve", stmp[:, pr, 0:n], stmp[:, pr, 0:n], rr, ALU.mult, waits=[a_tok])
                d_tok = tt(P, "dve", hid[:, slot, t0:t1], bu[:, 0:n], stmp[:, pr, 0:n], ALU.mult,
                           waits=[s_tok2, toks[1], hid_rd[b]])
                bank_free[2 * pr + 1] = d_tok
                hid_ready[(slot, tl)] = d_tok
                pe_last[0] = toks[1]

            def GU(g):
                jj0 = 0
                if g == 0 and tile_first:
                    reqs = []
                    for jj in range(2):
                        j = g * GJ + jj
                        reqs.append((w_request(("col", wg, j)), w_request(("col", wu, j))))
                    for ti, tl in enumerate(tiles):
                        if pre_tile is not None:
                            pre_tile(tl)
                        for jj in range(2):
                            GU_unit(g, jj, reqs[jj], ti, tl, True)
                    jj0 = 2
                for jj in range(jj0, GJ):
                    j = g * GJ + jj
                    req = (w_request(("col", wg, j)), w_request(("col", wu, j)))
                    for ti, tl in enumerate(tiles):
                        GU_unit(g, jj, req, ti, tl, False)

            def DN(g, final):
                b = g % 2
                tk = None
                for mq in range(4):
                    di, dslot, dtok = w_request(("down", wd, g, mq))
                    tk = DN_mq(g, dslot, dtok, mq)
                    w_release(di, tk)
                hid_rd[b] = tk
                pe_last[0] = tk

            def DN_last2(ga, gb):
                tk = None
                for mq in range(4):
                    reqs = [w_request(("down", wd, g_, mq)) for g_ in (ga, gb)]
                    for mm_ in range(4):
                        m = mq * 4 + mm_
                        for tl in tiles:
                            t0, t1 = tl
                            n = t1 - t0
                            bk = next_acc()
                            idx = 0
                            for (di, dslot, dtok), g_ in zip(reqs, (ga, gb)):
                                for jj in range(GJ):
                                    slot = (g_ % 2) * GJ + jj
                                    w = [hid_ready[(slot, tl)]]
                                    if jj == 0:
                                        w.append(dtok)
                                    if idx == 0:
                                        w.append(bank_free[bk])
                                    tk = mm(P, ps[bk][:, 0:n],
                                            ring[:, dslot, jj * 512 + mm_ * 128: jj * 512 + (mm_ + 1) * 128],
                                            hid[:, slot, t0:t1], idx == 0, idx == 2 * GJ - 1, sig=(idx == 2 * GJ - 1),
                                            waits=w)
                                    idx += 1
                            e_tok = stt(P, h[:, m, t0:t1], ps[bk][:, 0:n], 0.5, h[:, m, t0:t1], ALU.mult, ALU.add,
                                        waits=[tk, last_evac.get((m, tl))])
                            bank_free[bk] = e_tok
                            last_evac[(m, tl)] = e_tok
                            ssq_update(m, t0, t1, e_tok, m == KC - 1)
                            if prod_cb is not None:
                                prod_cb(m, tl, e_tok)
                    for (di, dslot, dtok) in reqs:
                        w_release(di, tk)
                hid_rd[0] = hid_rd[1] = tk
                pe_last[0] = tk

            def DN_mq(g, dslot, dtok, mq):
                b = g % 2
                tk = None
                for mm_ in range(4):
                    m = mq * 4 + mm_
                    for (t0, t1) in tiles:
                        n = t1 - t0
                        bk = next_acc()
                        for jj in range(GJ):
                            slot = b * GJ + jj
                            w = [hid_ready[(slot, (t0, t1))]]
                            if jj == 0:
                                w += [dtok, bank_free[bk]]
                            tk = mm(P, ps[bk][:, 0:n], ring[:, dslot, jj * 512 + mm_ * 128: jj * 512 + (mm_ + 1) * 128],
                                    hid[:, slot, t0:t1], jj == 0, jj == GJ - 1, sig=(jj == GJ - 1), waits=w)
                        e_tok = stt(P, h[:, m, t0:t1], ps[bk][:, 0:n], 0.5, h[:, m, t0:t1], ALU.mult, ALU.add,
                                    waits=[tk, last_evac.get((m, (t0, t1)))])
                        bank_free[bk] = e_tok
                        last_evac[(m, (t0, t1))] = e_tok
                return tk

            assert NG >= 2
            GU(0)
            for g in range(1, NG):
                GU(g)
                if g < NG - 1:
                    DN(g - 1, False)
            gu_last[0] = pe_last[0]
            DN_last2(NG - 2, NG - 1)
            if final_cb is not None:
                final_cb()
            return pe_last[0], last_evac

        xn_tok, rstd1 = {}, {}
        dgv = [dg[:, i, :, :].rearrange("p a b -> p (a b)") for i in range(2)]
        sq1_rd = [None, None]
        sq1d_rd = [None, None, None]
        bank1 = {tl: 4 + i for i, tl in enumerate(T1)}
        assert len(T1) <= 3
        HPE = KC // 2

        def rstd1_pre(tl):
            t0, t1 = tl
            n = t1 - t0
            bk = bank1[tl]
            for c in range(KC):
                xn_tok[(c, tl)] = xn_prod(c, t0, t1, 0, x_tok[tl], None, 0)
            m_tok = None
            for c in range(HPE):
                i = c % 2
                a_tok = act(P, dgv[i][:, 0:n].bitcast(F32R), h[:, c, t0:t1], AF.Square, waits=[x_tok[tl], sq1_rd[i]])
                w = [a_tok, c_ones]
                if c == 0:
                    w.append(bank_free[bk])
                m_tok = mm(P, ps[bk][:, 0:n], ones_r[:].bitcast(F32R), dgv[i][:, 0:n].bitcast(F32R), c == 0, False,
                           sig=True, waits=w)
                sq1_rd[i] = m_tok
            acc = statAB[0]
            d_tok = act(P, acc[:, t0:t1], h[:, HPE, t0:t1], AF.Square, waits=[x_tok[tl]])
            for c in range(HPE + 1, KC):
                i = c % 3
                buf = sqTT[0][i]
                a_tok = act(P, buf[:, 0:n], h[:, c, t0:t1], AF.Square, waits=[x_tok[tl], sq1d_rd[i]])
                d_tok = tt(P, "dve", acc[:, t0:t1], acc[:, t0:t1], buf[:, 0:n], ALU.add, waits=[a_tok, d_tok])
                sq1d_rd[i] = d_tok
            pend1[tl] = (d_tok, m_tok)

        pend1 = {}

        def rstd1_post(tl):
            t0, t1 = tl
            n = t1 - t0
            bk = bank1[tl]
            d_tok, m_tok = pend1.pop(tl)
            m_tok = mm(P, ps[bk][:, 0:n], ones[:], statAB[0][:, t0:t1], False, True, sig=True,
                       waits=[d_tok, m_tok, c_ones])
            s_tok = act(P, statB[:, t0:t1], ps[bk][:, 0:n], AF.Sqrt, scale=1.0 / D, bias=epsb[:, 0:1],
                        waits=[m_tok, c_eps])
            bank_free[bk] = s_tok
            rstd1[tl] = P.op("dve", lambda hh, o=statB[:, t0:t1]: hh.reciprocal(out=o, in_=o), sig=True, waits=[s_tok])

        ssq_reset(T1, None)
        xn_mix, rstd_mix = {}, {}

        def mix_prod(m, tl, e_tok):
            xn_mix[(m, tl)] = xn_prod(m, tl[0], tl[1], 1, e_tok, gu_last[0], 3)

        pe_tok, evac1 = ffn("f1", T1, xn_tok, rstd1, ("ffn1_w_gate", "ffn1_w_up", "ffn1_w_down"), None, None,
                            mix_prod, rstd1_post, True, rstd1_pre)
        ssq_flush(0)
        win_hooks = [(lambda tl=tl: rstd_mix.update(rstd_finish([tl], D))) for tl in T1]
        xn_tok = xn_mix

        hist_p = DmaSem(P, "hist_p")
        out_p = DmaSem(P, "out_p")
        hist_c = [DmaSem(P, "hist_c0"), DmaSem(P, "hist_c1")]
        out_c = [DmaSem(P, "out_c0"), DmaSem(P, "out_c1")]
        ysem = DmaSem(P, "ysem")

        def win_unit(specs, evac):
            reqs = [w_request(s) for s in specs]
            last = None
            for ti, (t0, t1) in enumerate(T1):
                n = t1 - t0
                pr = next_pair()
                banks = [2 * pr + i for i in range(len(reqs))]
                toks = []
                for (wi, s, wtok), bidx in zip(reqs, banks):
                    for kc in range(KC):
                        w = [xn_tok[(kc, (t0, t1))]]
                        if kc == 0:
                            w += [wtok, bank_free[bidx]]
                        tk = mm(P, ps[bidx][:, 0:n], ring[:, s, kc * 128:(kc + 1) * 128], xn[:, kc, t0:t1],
                                kc == 0, kc == KC - 1, sig=(kc == KC - 1), waits=w)
                    toks.append(tk)
                if ti == len(T1) - 1:
                    for (wi, s, wtok), tk in zip(reqs, toks):
                        w_release(wi, tk)
                if win_hooks:
                    win_hooks.pop(0)()
                evac(ti, t0, t1, banks, toks, pr)
                last = toks[-1]
            return last

        mixn = [v for v in xn_tok.values()]
        upe_free = [mixn, mixn]
        sab_rd = [mixn, mixn]
        pm_rd = [None]
        pm_ready = {}
        pend_pw = []
        cvo_guard = []
        p_tok = {}
        hist_pb = [hist_p, DmaSem(P, "hist_p1")]
        out_pb = [out_p, DmaSem(P, "out_p1")]

        for q in range(PCH):
            g = q // 2
            w_ = POOL_W[g]
            ub = q % 2
            upe = upes[ub]
            h_tok = hist_pb[ub].dma("sp", v3(upe[:, PW:UP])[:, :, 0:PHIST], v3(spT[q]), waits=[upe_free[ub]])
            ev = []

            def evac_pool(ti, t0, t1, banks, toks, pr, ev=ev, upe=upe, ub=ub):
                b = banks[0]
                pe = min(t1, PW)
                rt = rstd_mix[(t0, t1)]
                tk = tt(P, "dve", upe[:, t0:pe], ps[b][:, 0:pe - t0], statB[:, t0:pe], ALU.mult,
                        waits=[toks[0], upe_free[ub], rt])
                if t1 > PW:
                    tk = tt(P, "dve", v3(upe[:, PW:UP])[:, :, PHIST:LP], v3(ps[b][:, PW - t0:PW - t0 + NS]),
                            v3(statB[:, PW:N1]), ALU.mult, waits=[toks[0], rt])
                bank_free[b] = tk
                ev.append(tk)

            win_unit([("col", "w_in", 2 * CCH + q)], evac_pool)
            while pend_pw:
                pend_pw.pop(0)()
            ready = [ev, h_tok]
            o1 = out_pb[ub].dma("sp", o_npp[q], upe[:, PW - PHIST:PW], waits=ready)
            o2 = out_pb[ub].dma("sp", v3(o_nps[q]), v3(upe[:, PW:UP])[:, :, LP - PHIST:LP], waits=ready)
            src, sh, k_tok = upe, 1, ready
            bufs = [S_a, S_b]
            bi = 0
            while sh < w_:
                dst = bufs[bi]
                lo = 2 * sh - 1
                k_tok = tt(P, "dve", dst[:, lo:UP], src[:, lo:UP], src[:, lo - sh:UP - sh], ALU.add,
                           waits=[k_tok, sab_rd[bi]])
                src = dst
                sh *= 2
                bi ^= 1
            kcq = q % 2
            sab_last = bi ^ 1
            w1 = [k_tok, pm_rd[0]]
            t_a = stt(P, pm[:, kcq, 0:cfg.nown], src[:, HALO:PW], 1.0 / w_, upe[:, HALO:PW], ALU.mult, ALU.subtract, waits=w1)
            t_b = stt(P, v3(pm[:, kcq, cfg.nown:N2]), v3(src[:, PW:UP])[:, :, PHIST:LP], 1.0 / w_,
                      v3(upe[:, PW:UP])[:, :, PHIST:LP], ALU.mult, ALU.subtract, waits=w1)
            t_c = tt(P, "dve", tiny[:, 0:16], src[:, HALO:HALO + 16], ic_sb[:, g * 16:(g + 1) * 16], ALU.mult,
                     waits=[k_tok, par_tok])
            t_d = tt(P, "dve", pm[:, kcq, 0:16], tiny[:, 0:16], upe[:, HALO:HALO + 16], ALU.subtract, waits=[t_c, t_a])
            sab_rd[sab_last] = [t_a, t_b, t_d]
            upe_free[ub] = [t_a, t_b, t_d, o2]
            pm_ready[(g, kcq)] = [t_a, t_b, t_d]
            cvo_guard = [upe_free[0], upe_free[1]]
            if kcq == 1:
                def pool_w_mm(g=g):
                    wi, s_, wtok = w_request(("poolw", g))
                    tk = None
                    for dm in range(2):
                        qq = 2 * g + dm
                        for (a, b_), (t0, t1) in zip(C2, T2):
                            n = b_ - a
                            bk = next_acc()
                            for kc in range(2):
                                w = [pm_ready[(g, kc)]]
                                if kc == 0:
                                    w += [wtok, bank_free[bk]]
                                tk = mm(P, ps[bk][:, 0:n], ring[:, s_, kc * 256 + dm * 128: kc * 256 + (dm + 1) * 128],
                                        pm[:, kc, a:b_], kc == 0, kc == 1, sig=(kc == 1), waits=w)
                            e_tok = act(P, hid[:, qq, t0:t1], ps[bk][:, 0:n], AF.Copy,
                                        scale=cv_sb[:, 2 * CCH + qq:2 * CCH + qq + 1], waits=[tk, par_tok, pe_tok])
                            bank_free[bk] = e_tok
                            p_tok[(qq, (t0, t1))] = e_tok
                    w_release(wi, tk)
                    pm_rd[0] = tk
                pend_pw.append(pool_w_mm)

        uce_free = [None, None]
        assert CV % 2 == 0
        E_T = [(2 * a, 2 * b) for a, b in _split(0, CV // 2, 256)]
        conv_banks = [4, 5, 6]
        assert len(E_T) <= 3 and E_T[-1][0] <= cfg.nown
        dg_rd = [None, None]
        TG = 4
        conv_in = {}
        sq_rd = [None]
        cvk = lambda k: cvo[:, k * N2:(k + 1) * N2]
        cvo_done = {}
        ssq_last = [None]

        def conv_win(k):
            bf = k % 2
            ue = uce[:, bf, :]
            h_tok = hist_c[bf].dma("sp", v3(ue[:, PW:UC].bitcast(F32R))[:, :, 0:HALO], v3(scT[k].bitcast(F32R)),
                                   waits=[uce_free[bf]])
            ev = []

            def evac_conv(ti, t0, t1, banks, toks, pr):
                ba, bg = banks
                n = t1 - t0
                rr = statB[:, t0:t1]
                g_tok = tt(P, "dve", stmp[:, pr, 0:n], ps[bg][:, 0:n], rr, ALU.mult,
                           waits=[toks[1], bank_free[ba], rstd_mix[(t0, t1)]])
                bank_free[bg] = g_tok
                a_tok0 = act(P, stmp[:, pr, 0:n], stmp[:, pr, 0:n], AF.Sigmoid, waits=[g_tok])
                a_tok = tt(P, "dve", stmp[:, pr, 0:n], stmp[:, pr, 0:n], rr, ALU.mult, waits=[a_tok0])
                pe = min(t1, PW)
                tk = tt(P, "dve", ue[:, t0:pe].bitcast(F32R), ps[ba][:, 0:pe - t0], stmp[:, pr, 0:pe - t0], ALU.mult,
                        waits=[a_tok, toks[0], uce_free[bf]])
                if t1 > PW:
                    o = PW - t0
                    tk = tt(P, "dve", v3(ue[:, PW:UC].bitcast(F32R))[:, :, HALO:LC], v3(ps[ba][:, o:o + NS]),
                            v3(stmp[:, pr, o:o + NS]), ALU.mult, waits=[a_tok, toks[0]])
                bank_free[ba] = tk
                ev.append(tk)

            pe_win = win_unit([("col", "w_in", k), ("col", "w_in", CCH + k)], evac_conv)
            ready = [ev, h_tok]
            o1 = out_c[bf].dma("sp", o_ncp[k], ue[:, PW - HALO:PW], waits=ready)
            o2 = out_c[bf].dma("sp", v3(o_ncs[k]), v3(ue[:, PW:UC])[:, :, LC - HALO:LC], waits=ready)
            conv_in[k] = (ready, [o2], pe_win)

        NGRP = -(-CONV_K // TG)
        assert NGRP % 2 == 0
        prebuilt = {}

        def diag_build(k, gi):
            b = gi % 2
            taps = list(range(gi * TG, min(gi * TG + TG, CONV_K)))
            bt = None
            for jj, i in enumerate(taps):
                bt = act(P, dg[:, b, jj, :].bitcast(F32R), ident_sb[:], AF.Copy,
                         scale=cw_sb[:, k * CONV_K + i:k * CONV_K + i + 1], waits=[dg_rd[b], par_tok])
            return bt

        def conv_prebuild(k):
            for gi in range(2):
                prebuilt[(k, gi)] = diag_build(k, gi)

        def conv_pe(k):
            bf = k % 2
            ue = uce[:, bf, :]
            ready, outs, _ = conv_in[k]
            tk = None
            for gi, j0 in enumerate(range(0, CONV_K, TG)):
                taps = list(range(j0, min(j0 + TG, CONV_K)))
                b = gi % 2
                bt = prebuilt.pop((k, gi), None)
                if bt is None:
                    bt = diag_build(k, gi)
                for jj, i in enumerate(taps):
                    for (e0, e1), bk in zip(E_T, conv_banks):
                        w = [bt]
                        if i == 0:
                            w += [ready, bank_free[bk]]
                        lastt = (jj == len(taps) - 1 and (e0, e1) == E_T[-1])
                        tk = mm(P, ps[bk][:, 0:e1 - e0], dg[:, b, jj, :].bitcast(F32R),
                                ue[:, e0 + i:e1 + i].bitcast(F32R), i == 0, i == CONV_K - 1, sig=lastt, waits=w)
                dg_rd[b] = tk
            uce_free[bf] = [tk, outs]
            ck = cvk(k)
            bcol = cv_sb[:, k:k + 1]
            ev = []
            for (e0, e1), bk in zip(E_T, conv_banks):
                pe = min(e1, cfg.nown)
                t_ = None
                if e0 < cfg.nown:
                    t_ = ts(P, "dve", ck[:, e0:pe], ps[bk][:, 0:pe - e0], bcol, None, ALU.add,
                            waits=[tk, par_tok, cvo_guard])
                if e1 == CV:
                    o = cfg.nown - e0
                    t_ = ts(P, "dve", v3(ck[:, cfg.nown:N2]), v3(ps[bk][:, o:o + NSTR * LC])[:, :, HALO:LC], bcol, None,
                            ALU.add, waits=[tk, par_tok, cvo_guard])
                bank_free[bk] = t_
                ev.append(t_)
            cvo_done[k] = ev
            if k == 0:
                ssq_last[0] = act(P, ssqc[:, 0:N2], ck[:, 0:N2], AF.Square, waits=[ev])
            else:
                s1 = act(P, pmf[:, 0:N2], ck[:, 0:N2], AF.Square, waits=[ev, sq_rd[0], pm_rd[0]])
                ssq_last[0] = tt(P, "dve", ssqc[:, 0:N2], ssqc[:, 0:N2], pmf[:, 0:N2], ALU.add, waits=[s1, ssq_last[0]])
                sq_rd[0] = ssq_last[0]

        conv_win(0)
        while pend_pw:
            pend_pw.pop(0)()
        for k in range(CCH):
            conv_prebuild(k)
            if k + 1 < CCH:
                conv_win(k + 1)
            conv_pe(k)
        pe_win_last = conv_in[CCH - 1][2]

        pe_conv_last = uce_free[(CCH - 1) % 2][0]
        evac_wo = {}
        pe_wo = [None]

        def wout_group(half, mp, s_, wtok, tl, with_ssq):
            t0, t1 = tl
            n = t1 - t0
            tk = None
            for mm_ in range(2):
                m = mp * 2 + mm_
                bk = next_acc()
                for kc in range(CCH):
                    if half == 0:
                        rhs, w = xn[:, kc, t0:t1], [c_tok[(kc, tl)]]
                    else:
                        rhs, w = hid[:, kc, t0:t1], [p_tok[(kc, tl)]]
                    if kc == 0:
                        w += [wtok, bank_free[bk]]
                    tk = mm(P, ps[bk][:, 0:n], ring[:, s_, kc * 256 + mm_ * 128: kc * 256 + (mm_ + 1) * 128], rhs,
                            kc == 0, kc == CCH - 1, sig=(kc == CCH - 1), waits=w)
                e_tok = tt(P, "dve", h[:, m, t0:t1], ps[bk][:, 0:n], h[:, m, t0:t1], ALU.add,
                           waits=[tk, evac_wo.get((m, tl))] + [evac1[k_] for k_ in evac1 if k_[0] == m])
                bank_free[bk] = e_tok
                evac_wo[(m, tl)] = e_tok
                if with_ssq:
                    ssq_update(m, t0, t1, e_tok, m == KC - 1)
                    if m >= CCH:
                        xn_f2[(m, tl)] = xn_prod(m, t0, t1, 2, e_tok, None, 3)
                    else:
                        xn_late.append((m, tl, e_tok))
            return tk

        xn_f2 = {}
        xn_late = []

        def wout_pass(half, mps, with_ssq=False, after_group=None):
            for mp in mps:
                wi, s_, wtok = w_request(("wout", half, mp))
                tk = None
                for tl in T2:
                    tk = wout_group(half, mp, s_, wtok, tl, with_ssq)
                    if after_group is not None:
                        after_group()
                w_release(wi, tk)
                pe_wo[0] = tk

        wout_pass(1, range(0, 1))

        c_rstd = []
        chain = []
        for (a, b_) in C2:
            bk = next_acc()
            m_tok = mm(P, ps[bk][:, 0:b_ - a], ones[:], ssqc[:, a:b_], True, True, sig=True,
                       waits=[ssq_last[0], c_ones, bank_free[bk]])
            s_tok = act(P, statB[:, a:b_], ps[bk][:, 0:b_ - a], AF.Sqrt, scale=1.0 / (CCH * 128), bias=epsb[:, 0:1],
                        waits=[m_tok, c_eps, ssq_last[0]])
            bank_free[bk] = s_tok

            def recip(a=a, b_=b_, s_tok=s_tok):
                c_rstd.append(P.op("dve", lambda hh, o=statB[:, a:b_]: hh.reciprocal(out=o, in_=o), sig=True,
                                   waits=[s_tok]))
            chain.append(recip)
        c_tok = {}

        def make_c(k):
            gcol = cv_sb[:, CCH + k:CCH + k + 1]
            ck = cvk(k)
            n1 = stt(P, ck[:, 0:N2], ck[:, 0:N2], gcol, statB[:, 0:N2], ALU.mult, ALU.mult, waits=[c_rstd, par_tok])
            a1 = act(P, xn[:, k, HALO:N1], ck[:, 0:N2], AF.Silu, waits=[n1, pe_win_last])
            for tl in T2:
                c_tok[(k, tl)] = [a1]
        for k in range(CCH):
            chain.append(lambda k=k: make_c(k))

        wout_pass(1, range(1, KC // 2), False, lambda: chain.pop(0)() if chain else None)
        while chain:
            chain.pop(0)()
        ssq_reset(T2, [pe_conv_last, [v for v in c_tok.values()]])
        wout_pass(0, range(0, KC // 2), True)
        for (m_, tl_, e_) in xn_late:
            xn_f2[(m_, tl_)] = xn_prod(m_, tl_[0], tl_[1], 2, e_, pe_wo[0], 3)
        ssq_flush(0)
        ssq_f2 = dict(ssq["last"])
        rstd_f2 = {}

        def rstd2_hook(tl):
            keep = ssq["last"]
            ssq["last"] = dict(ssq_f2)
            rstd_f2.update(rstd_finish([tl], D))
            ssq["last"] = keep


        ssq_reset(T2, None)
        yTv = yT.rearrange("c p t -> p c t")
        y_last = [None]

        fin_scaled = {}

        def fin_prod(m, tl, e_tok):
            fin_scaled[(m, tl)] = act(P, h[:, m, tl[0]:tl[1]], h[:, m, tl[0]:tl[1]], AF.Copy,
                                      scale=g_sb[:, 3 * KC + m:3 * KC + m + 1], waits=[e_tok, par_tok])

        def fin_cb():
            half = KC // 2
            for tl in T2:
                r_tok = rstd_finish([tl], D)[tl]
                toks = []
                for c in range(KC):
                    toks.append(tt(P, "dve", h[:, c, tl[0]:tl[1]], h[:, c, tl[0]:tl[1]], statB[:, tl[0]:tl[1]], ALU.mult,
                                   waits=[r_tok, fin_scaled[(c, tl)]]))
                    if c == half - 1 or c == KC - 1:
                        c0 = 0 if c == half - 1 else half
                        y_last[0] = ysem.dma("sp", yTv[:, c0:c + 1, tl[0] - HALO:tl[1] - HALO],
                                             h[:, c0:c + 1, tl[0]:tl[1]], waits=toks[c0:c + 1])

        pe_tok2, evac2 = ffn("f2", T2, xn_f2, rstd_f2, ("ffn2_w_gate", "ffn2_w_up", "ffn2_w_down"), fin_cb,
                             [v for v in c_tok.values()], fin_prod, rstd2_hook)
        last = y_last[0]
        P.wait("sp", last)
        for dsm in (out_pb[0], out_pb[1], out_c[0], out_c[1]):
            P.wait("sp", Tok(dsm.sem, 16 * dsm.n, ("dma", id(dsm.sem))))
        assert len(plan) == n_w, (len(plan), n_w)
        P.run(block)
    return nc, plan


def _col_tile(W, m):
    blk = W[:, m * 128:(m + 1) * 128].reshape(KC, 128, 128)
    return np.ascontiguousarray(blk.transpose(1, 0, 2)).reshape(128, 2048)


def _down_tile(Wd, g, mq):
    blk = Wd[g * GJ * 128:(g + 1) * GJ * 128, mq * 512:(mq + 1) * 512].reshape(GJ, 128, 512)
    return np.ascontiguousarray(blk.transpose(1, 0, 2)).reshape(128, 2048)


def _wout_tile(W, half, mp):
    blk = W[half * 1024:(half + 1) * 1024, mp * 256:(mp + 1) * 256].reshape(CCH, 128, 256)
    return np.ascontiguousarray(blk.transpose(1, 0, 2)).reshape(128, 2048)


def _poolw_tile(pw, g):
    t = np.zeros((128, 2048), np.float32)
    blk = pw[g].reshape(2, 128, 256).transpose(1, 0, 2).reshape(128, 512)
    t[:, :512] = blk
    return t


def _pack_weights(plan, W):
    ws = np.empty((len(plan), 128, 2048), np.float32)
    for i, spec in enumerate(plan):
        if spec[0] == "col":
            ws[i] = _col_tile(W[spec[1]], spec[2])
        elif spec[0] == "down":
            ws[i] = _down_tile(W[spec[1]], spec[2], spec[3])
        elif spec[0] == "wout":
            ws[i] = _wout_tile(W["w_out"], spec[1], spec[2])
        else:
            ws[i] = _poolw_tile(W["pool_w"], spec[1])
    return ws


def _chunks(v, n):
    return np.ascontiguousarray(np.asarray(v, np.float32).reshape(n, 128).T)


_CACHE = {}


def run(inputs, cfg):
    key = (cfg.dff, cfg.nown, tuple(cfg.T1))
    if key not in _CACHE:
        _CACHE[key] = build_program(cfg)
    nc, plan = _CACHE[key]
    f = lambda k: np.asarray(inputs[k], np.float32)
    W = {k: f(k)[0] for k in ("ffn1_w_gate", "ffn1_w_up", "ffn1_w_down", "w_in", "w_out",
                              "ffn2_w_gate", "ffn2_w_up", "ffn2_w_down", "pool_w")}
    ws = _pack_weights(plan, W)
    gains = np.concatenate([_chunks(f("ffn1_norm")[0], KC), _chunks(f("mix_norm")[0], KC),
                            _chunks(f("ffn2_norm")[0], KC), _chunks(f("final_norm"), KC)], axis=1)
    cwv = f("conv_w")[0]
    cw = np.ascontiguousarray(cwv.T.reshape(CCH, 128, CONV_K).transpose(1, 0, 2)).reshape(128, CCH * CONV_K)
    cvec = np.concatenate([_chunks(f("conv_b")[0], CCH), _chunks(f("conv_norm")[0], CCH),
                           _chunks(f("pool_scale")[0], PCH)], axis=1)
    seq = np.concatenate([f("meta_tokens"), f("x_prompt")[0]], axis=0)
    nown = cfg.nown
    assert seq.shape[0] == NCORES * nown
    xs = f("x_sample")
    sc = f("state_conv")[0]
    sp = f("state_pool")[0]
    in_maps = []
    for c in range(NCORES):
        rows = np.zeros((cfg.N1, D), np.float32)
        lo = c * nown - HALO
        if lo >= 0:
            rows[0:cfg.PW] = seq[lo:lo + cfg.PW]
        else:
            rows[HALO:cfg.PW] = seq[0:nown]
        rows[cfg.PW:] = xs[c * NSTR:(c + 1) * NSTR].reshape(NS, D)
        xt3 = rows.T.reshape(KC, 128, cfg.N1).transpose(1, 0, 2)
        xT = np.concatenate([xt3[:, :, a:b].reshape(128, -1) for a, b in cfg.T1], axis=1)
        xT = np.ascontiguousarray(xT)
        icnt = np.empty((128, 4, 16), np.float32)
        for g, w_ in enumerate(POOL_W):
            for i in range(16):
                cnt = min(i + 1, w_) if c == 0 else w_
                icnt[:, g, i] = np.float32(1.0) / np.float32(cnt)
        scT = np.ascontiguousarray(sc[c * NSTR:(c + 1) * NSTR].reshape(NSTR, HALO, CCH, 128).transpose(2, 3, 0, 1))
        spT = np.ascontiguousarray(sp[c * NSTR:(c + 1) * NSTR].reshape(NSTR, PHIST, PCH, 128).transpose(2, 3, 0, 1))
        in_maps.append({
            "xT": xT, "ws": ws, "gains": gains, "cw": cw, "cvec": cvec,
            "icnt": icnt.reshape(128, 64), "ident": np.eye(128, dtype=np.float32),
            "scT": scT.reshape(CCH, 128, NSTR * HALO), "spT": spT.reshape(PCH, 128, NSTR * PHIST),
        })
    res = run_bass_kernel_spmd(nc, in_maps, core_ids=list(range(NCORES)))
    outs = res.results
    yp, ys = [], []
    for c in range(NCORES):
        y = np.asarray(outs[c]["yT"], np.float32).reshape(D, cfg.N2)
        yp.append(y[:, :nown].T)
        ys.append(y[:, nown:].reshape(D, NSTR, DSEQ).transpose(1, 2, 0))
    y_prompt = np.concatenate(yp, axis=0)[N_META:][None]
    y_sample = np.concatenate(ys, axis=0)
    last = outs[NCORES - 1]
    ncp = np.asarray(last["ncp"], np.float32).reshape(CCH * 128, HALO).T[None, None]
    npp = np.asarray(last["npp"], np.float32).reshape(PCH * 128, PHIST).T[None, None]
    ncs = np.concatenate([np.asarray(outs[c]["ncs"], np.float32).reshape(CCH * 128, NSTR, HALO).transpose(1, 2, 0)
                          for c in range(NCORES)], axis=0)[None]
    nps = np.concatenate([np.asarray(outs[c]["nps"], np.float32).reshape(PCH * 128, NSTR, PHIST).transpose(1, 2, 0)
                          for c in range(NCORES)], axis=0)[None]
    return (np.ascontiguousarray(y_prompt), np.ascontiguousarray(y_sample), np.ascontiguousarray(ncp),
            np.ascontiguousarray(npp), np.ascontiguousarray(ncs), np.ascontiguousarray(nps))


def kernel(**inputs):
    return run(inputs, Cfg())
```
